# Optimizing a Trainium2 kernel written in Bass

```python
import jax, jax.numpy as jnp
from jax import lax
import numpy as np

D_MODEL = 1024
BATCH = 2
SEQ = 8192
DEPTH = 1

HEAD_DIM = 64
D_MIX = D_MODEL
SWA_WIDTH = D_MIX // 2
SWA_Q_HEADS = SWA_WIDTH // HEAD_DIM
SWA_KV_HEADS = 2
SWA_GROUP = SWA_Q_HEADS // SWA_KV_HEADS
DSA_WIDTH = D_MIX - SWA_WIDTH
DSA_Q_HEADS = DSA_WIDTH // HEAD_DIM
IDX_HEADS = 4
IDX_DIM = 64
WINDOW = 128
BLOCK = 128
TOPK_MAX = 256
ROPE_THETA = 10000.0
LN_EPS = 1e-5
ALPHA = (2.0 * DEPTH) ** 0.25
BETA = (8.0 * DEPTH) ** -0.25
COLUMN_SIZES = (
    SWA_Q_HEADS * HEAD_DIM,
    SWA_KV_HEADS * HEAD_DIM,
    SWA_KV_HEADS * HEAD_DIM,
    SWA_WIDTH,
    DSA_Q_HEADS * HEAD_DIM,
    HEAD_DIM,
    HEAD_DIM,
    DSA_WIDTH,
    IDX_HEADS * IDX_DIM,
    IDX_DIM,
    IDX_HEADS,
)
VALUE_COLUMNS = (2, 6)
N_COLS = sum(COLUMN_SIZES)

kernel_name = "hybrid_swa_sink_dsa_deepnorm_layer"


def rope(x, pos):
    d = x.shape[-1]
    inv = ROPE_THETA ** (-jnp.arange(0, d, 2, dtype=jnp.float32) / d)
    ang = pos.astype(jnp.float32)[:, :, None, None] * inv
    cos, sin = jnp.cos(ang), jnp.sin(ang)
    x1, x2 = jnp.split(x.astype(jnp.float32), 2, axis=-1)
    return jnp.concatenate([x1 * cos - x2 * sin, x2 * cos + x1 * sin], axis=-1).astype(x.dtype)


def layer_norm(x, g, b):
    xf = x.astype(jnp.float32)
    mu = jnp.mean(xf, axis=-1, keepdims=True)
    var = jnp.mean(jnp.square(xf - mu), axis=-1, keepdims=True)
    y = (xf - mu) * lax.rsqrt(var + LN_EPS) * g.astype(jnp.float32) + b.astype(jnp.float32)
    return y.astype(x.dtype)


def sliding_window_sink_attention(q, k, v, sinks):
    b, s, _, d = q.shape
    nb = s // BLOCK
    qb = q.reshape(b, nb, BLOCK, SWA_KV_HEADS, SWA_GROUP, d)

    def with_prev(t):
        t = t.reshape(b, nb, BLOCK, SWA_KV_HEADS, d)
        prev = jnp.pad(t[:, :-1], ((0, 0), (1, 0), (0, 0), (0, 0), (0, 0)))
        return jnp.concatenate([prev, t], axis=2)

    kk, vv = with_prev(k), with_prev(v)
    logits = jnp.einsum('bnqgrd,bnkgd->bngrqk', qb, kk).astype(jnp.float32) * (d ** -0.5)
    blk = jnp.arange(nb)[:, None, None]
    qpos = blk * BLOCK + jnp.arange(BLOCK)[None, :, None]
    kpos = (blk - 1) * BLOCK + jnp.arange(2 * BLOCK)[None, None, :]
    rel = qpos - kpos
    mask = (rel >= 0) & (rel < WINDOW) & (kpos >= 0)
    logits = jnp.where(mask[None, :, None, None], logits, -jnp.inf)
    sink = jnp.broadcast_to(
        sinks.astype(jnp.float32).reshape(SWA_KV_HEADS, SWA_GROUP)[None, None, :, :, None, None],
        logits.shape[:-1] + (1,))
    probs = jax.nn.softmax(jnp.concatenate([logits, sink], axis=-1), axis=-1)[..., :-1]
    out = jnp.einsum('bngrqk,bnkgd->bnqgrd', probs.astype(v.dtype), vv)
    return out.reshape(b, s, SWA_Q_HEADS * d)


def dsa_sparse_attention(q, k, v, q_idx, k_idx, w_idx):
    b, s, h, d = q.shape
    n_sel = min(TOPK_MAX, s // 4)
    nb = s // BLOCK
    key_pos = jnp.arange(s)

    def to_blocks(t):
        return jnp.moveaxis(t.reshape((b, nb, BLOCK) + t.shape[2:]), 1, 0)

    def one_block(args):
        qb, qib, wb, tpos = args
        rel = jax.nn.relu(jnp.einsum('bqhd,bsd->bqhs', qib, k_idx).astype(jnp.float32))
        score = jnp.einsum('bqhs,bqh->bqs', rel, wb.astype(jnp.float32))
        causal = key_pos[None, :] <= tpos[:, None]
        score = jnp.where(causal[None], score, -jnp.inf)
        _, idx = lax.top_k(score, n_sel)
        k_sel = jax.vmap(lambda kk, ii: kk[ii])(k, idx)
        v_sel = jax.vmap(lambda vv, ii: vv[ii])(v, idx)
        logits = jnp.einsum('bqhd,bqkd->bhqk', qb, k_sel).astype(jnp.float32) * (d ** -0.5)
        valid = idx <= tpos[None, :, None]
        logits = jnp.where(valid[:, None], logits, -jnp.inf)
        p = jax.nn.softmax(logits, axis=-1).astype(v.dtype)
        return jnp.einsum('bhqk,bqkd->bqhd', p, v_sel)

    qpos = jnp.arange(s).reshape(nb, BLOCK)
    out = lax.map(one_block, (to_blocks(q), to_blocks(q_idx), to_blocks(w_idx), qpos))
    return jnp.moveaxis(out, 0, 1).reshape(b, s, h * d)


def hybrid_layer(x, positions, w_in, b_in, sinks, w_out, b_out, ln_gain, ln_bias):
    b, s, _ = x.shape
    hcat = jnp.einsum('bsd,dc->bsc', x, w_in) + b_in
    splits = np.cumsum(COLUMN_SIZES)[:-1].tolist()
    aq, ak, av, ag, bq, bk, bv, bg, iq, ik, iw = jnp.split(hcat, splits, axis=-1)
    aq = rope(aq.reshape(b, s, SWA_Q_HEADS, HEAD_DIM), positions)
    ak = rope(ak.reshape(b, s, SWA_KV_HEADS, HEAD_DIM), positions)
    av = av.reshape(b, s, SWA_KV_HEADS, HEAD_DIM)
    a_out = sliding_window_sink_attention(aq, ak, av, sinks) * jax.nn.silu(ag)
    bq = rope(bq.reshape(b, s, DSA_Q_HEADS, HEAD_DIM), positions)
    bk = rope(bk.reshape(b, s, 1, HEAD_DIM), positions)[:, :, 0]
    iq = rope(iq.reshape(b, s, IDX_HEADS, IDX_DIM), positions)
    ik = rope(ik.reshape(b, s, 1, IDX_DIM), positions)[:, :, 0]
    iw = iw * (IDX_HEADS ** -0.5 * IDX_DIM ** -0.5)
    b_mix = dsa_sparse_attention(bq, bk, bv, iq, ik, iw) * jax.nn.silu(bg)
    y = jnp.einsum('bsc,cd->bsd', jnp.concatenate([a_out, b_mix], axis=-1), w_out) + b_out
    return layer_norm(ALPHA * x + y, ln_gain, ln_bias)


def setup_inputs(seed: int = 0) -> dict:
    key = jax.random.key(seed)
    ks = jax.random.split(key, 9)
    x = jax.random.normal(ks[0], (BATCH, SEQ, D_MODEL), jnp.float32)
    positions = jnp.broadcast_to(jnp.arange(SEQ, dtype=jnp.int32), (BATCH, SEQ))
    col_scale = jnp.concatenate([
        jnp.full((n,), BETA if i in VALUE_COLUMNS else 1.0, jnp.float32)
        for i, n in enumerate(COLUMN_SIZES)])
    w_in = jax.random.normal(ks[1], (DEPTH, D_MODEL, N_COLS), jnp.float32) * (D_MODEL ** -0.5) * col_scale
    b_in = 0.02 * jax.random.normal(ks[2], (DEPTH, N_COLS), jnp.float32)
    swa_sinks = jax.random.normal(ks[3], (DEPTH, SWA_Q_HEADS), jnp.float32)
    w_out = jax.random.normal(ks[4], (DEPTH, D_MIX, D_MODEL), jnp.float32) * (D_MIX ** -0.5) * BETA
    b_out = 0.02 * jax.random.normal(ks[5], (DEPTH, D_MODEL), jnp.float32)
    ln_gain = 1.0 + 0.02 * jax.random.normal(ks[6], (DEPTH, D_MODEL), jnp.float32)
    ln_bias = 0.02 * jax.random.normal(ks[7], (DEPTH, D_MODEL), jnp.float32)
    return {"x": x, "positions": positions, "w_in": w_in, "b_in": b_in,
            "swa_sinks": swa_sinks, "w_out": w_out, "b_out": b_out,
            "ln_gain": ln_gain, "ln_bias": ln_bias}


def reference(x, positions, w_in, b_in, swa_sinks, w_out, b_out, ln_gain, ln_bias):
    h = x
    for layer in range(DEPTH):
        h = hybrid_layer(h, positions, w_in[layer], b_in[layer], swa_sinks[layer],
                         w_out[layer], b_out[layer], ln_gain[layer], ln_bias[layer])
    return h
```

```python
import math
from contextlib import ExitStack
import numpy as np
import concourse.bass as bass
import concourse.mybir as mybir
from concourse.bass_utils import run_bass_kernel_spmd

F32 = mybir.dt.float32
BF16 = mybir.dt.bfloat16
I32 = mybir.dt.int32
U32 = mybir.dt.uint32
ALU = mybir.AluOpType
AF = mybir.ActivationFunctionType
ENGS = ("pe", "act", "dve", "pool", "sp")

S = 8192
NOWN = 2048
ALPHA = 2.0 ** 0.25
LN_EPS = 1e-5
NBIS = 8
TWO_PI = 2.0 * math.pi
C1 = 6.28125
C2 = TWO_PI - 6.28125
SHR = 1.0 - 2e-6
PI32 = float(np.float32(math.pi))
C_INV, C_SGN, C_HPI, C_EPS, C_CNEG, C_SWM, C_SWMF, C_END = 0, 1, 2, 3, 515, 1027, 1283, 1539


class Tl:
    def __init__(self, ap, keys):
        self.ap = ap
        self.keys = list(keys)

    def __getitem__(self, idx):
        return self.ap[idx]


class Prog:
    def __init__(self, nc, n_dma_sems=8):
        self.nc = nc
        self.eng = {"pe": nc.tensor, "act": nc.scalar, "dve": nc.vector,
                    "pool": nc.gpsimd, "sp": nc.sync}
        self.ops = []
        self.last_w = {}
        self.readers = {}
        self.n_dma_sems = n_dma_sems

    @staticmethod
    def _keys(lst):
        out = []
        for k in lst:
            if isinstance(k, Tl):
                out.extend(k.keys)
            elif isinstance(k, (list, tuple)) and len(k) and isinstance(k[0], (list, tuple, Tl)):
                out.extend(Prog._keys(k))
            else:
                out.append(k)
        return out

    def op(self, eng, fn, reads=(), writes=(), dma=False):
        reads = self._keys(reads)
        writes = self._keys(writes)
        deps = set()
        for k in reads:
            if k in self.last_w:
                deps.add(self.last_w[k])
        for k in writes:
            if k in self.last_w:
                deps.add(self.last_w[k])
            for r in self.readers.get(k, ()):
                deps.add(r)
        oid = len(self.ops)
        self.ops.append([eng, fn, sorted(deps), dma])
        for k in reads:
            self.readers.setdefault(k, []).append(oid)
        for k in writes:
            self.last_w[k] = oid
            self.readers[k] = []
        return oid

    def emit(self, stack):
        nc = self.nc
        ops = self.ops
        n = len(ops)
        needed = [False] * n
        for rec in ops:
            eng, fn, deps, dma = rec
            latest = {}
            keep = []
            for d in deps:
                deng, _, _, ddma = ops[d]
                if ddma:
                    keep.append(d)
                    continue
                if deng == "pe" and eng == "pe":
                    continue
                if deng not in latest or d > latest[deng]:
                    latest[deng] = d
            keep.extend(latest.values())
            rec[2] = sorted(keep)
            for d in keep:
                needed[d] = True
        sems = {e: stack.enter_context(nc.semaphore("s_" + e)) for e in ENGS}
        dma_sems = {e: [stack.enter_context(nc.semaphore(f"d_{e}_{i}"))
                        for i in range(self.n_dma_sems)] for e in ("sp", "pool", "act")}
        sig_count = {e: 0 for e in ENGS}
        dma_count = {e: 0 for e in ENGS}
        handle = [None] * n
        waited = {e: {} for e in ENGS}
        last_dma = {}

        def wait(e, sem, val):
            key = id(sem)
            if waited[e].get(key, 0) >= val:
                return
            self.eng[e].wait_ge(sem, val)
            waited[e][key] = val

        for i, (eng, fn, deps, dma) in enumerate(ops):
            for d in deps:
                deng, _, _, ddma = ops[d]
                if deng == "pe" and eng == "pe" and not ddma:
                    continue
                h = handle[d]
                wait(eng, h[0], h[1])
            if dma:
                k = dma_count[eng]
                dma_count[eng] += 1
                sem = dma_sems[eng][k % self.n_dma_sems]
                rnd = k // self.n_dma_sems
                if rnd > 0:
                    wait(eng, sem, 16 * rnd)
                ins = fn(self.eng[eng])
                ins.then_inc(sem, 16)
                handle[i] = (sem, 16 * (rnd + 1))
                last_dma[(eng, k % self.n_dma_sems)] = handle[i]
            else:
                ins = fn(self.eng[eng])
                if needed[i]:
                    sig_count[eng] += 1
                    ins.then_inc(sems[eng], 1)
                    handle[i] = (sems[eng], sig_count[eng])
        for (eng, _), h in last_dma.items():
            wait(eng, h[0], h[1])
        return sig_count, dma_count


def bc_mid(ap2d, n):
    return ap2d.unsqueeze(1).to_broadcast([ap2d.shape[0], n, ap2d.shape[1]])


def build():
    nc = bass.Bass("TRN2", target_bir_lowering=False, dynamic_dma_scratch_size=4096)

    def D(name, shape, dt, kind="ExternalInput"):
        return nc.dram_tensor(name, shape, dt, kind=kind).ap()

    xg = D("xg", [8, 128, S], F32)
    xo = D("xo", [8, 128, NOWN], F32)
    xp = D("xp", [8, 128, NOWN], F32)
    xtok = D("xtok", [NOWN, 1024], F32)
    posg = D("posg", [1, S], I32)
    poso = D("poso", [1, NOWN], I32)
    posp = D("posp", [1, NOWN], I32)
    wT = D("wT", [34, 128, 1024], F32)
    biasT = D("biasT", [128, 34], F32)
    wvg = D("wvg", [128, 8 * 64], F32)
    wvo = D("wvo", [128, 8 * 132], F32)
    bvg = D("bvg", [1, 64], F32)
    bvo = D("bvo", [1, 132], F32)
    wo = D("wo", [128, 8 * 1024], F32)
    bo = D("bo", [1, 1024], F32)
    lng = D("lng", [1, 1024], F32)
    lnb = D("lnb", [1, 1024], F32)
    sinks = D("sinks", [1, 8], F32)
    cst = D("cst", [128, C_END], F32)
    out = D("out", [NOWN, 1024], F32, kind="ExternalOutput")
    wbf = D("wbf", [34, 128, 1024], BF16, kind="Internal")

    with ExitStack() as st:
        def T(name, shape, dt):
            t = st.enter_context(nc.sbuf_tensor(name, shape, dt))
            return Tl(t[:], [name])

        KT = T("KT", [128, S], BF16)
        Vb = T("Vb", [128, 65, 129], BF16)
        wo_t = T("wo_t", [128, 8, 1024], BF16)
        bo_t = T("bo_t", [128, 1024], BF16)
        lng_t = T("lng_t", [128, 1024], F32)
        lnb_t = T("lnb_t", [128, 1024], F32)
        cst_t = T("cst_t", [128, C_END], F32)
        bias_t = T("bias_t", [128, 34], F32)
        ident = T("ident", [128, 128], BF16)
        ones_t = T("ones_t", [128, 128], BF16)
        onesf = T("onesf", [128, 64], F32)
        wvg_t = T("wvg_t", [128, 8, 64], BF16)
        wvo_t = T("wvo_t", [128, 8, 132], BF16)
        bvg_t = T("bvg_t", [128, 64], F32)
        bvo_t = T("bvo_t", [128, 132], F32)
        snk = T("snk", [128, 4], F32)
        bqTs = [T(f"bqT{p}", [128, 8, 512], BF16) for p in range(2)]
        iqT = T("iqT", [128, 4, 512], BF16)
        bgTs = [T(f"bgT{p}", [128, 4, 512], BF16) for p in range(2)]
        mixA = T("mixA", [128, 4, 512], BF16)
        wtok = T("wtok", [128, 4, 4], F32)
        Wb = [T(f"Wb{i}", [128, 8, 128], BF16) for i in range(4)]
        work = [T(f"work{i}", [128, 128], F32) for i in range(2)]
        M_all = st.enter_context(nc.sbuf_tensor("M_all", [128, S], BF16))
        MA = [Tl(M_all[:, kc * 512:(kc + 1) * 512], [("M", kc)]) for kc in range(16)]
        Sel = T("Sel", [128, 128], F32)
        negbig = T("negbig", [128, 1], F32)
        mskb = T("mskb", [128, 512], BF16)
        neghalf = T("neghalf", [128, 1], F32)
        Rt = T("Rt", [128, 512], F32)
        Rbc = T("Rbc", [128, 512], F32)
        junkA = T("junkA", [128, 1024], BF16)
        zt = T("zt", [128, 1024], F32)
        mixB = T("mixB", [128, 4, 128], BF16)
        Dg = [T(f"Dg{i}", [128, 4, 128], BF16) for i in range(2)]
        smA = [T(f"smA{i}", [128, 160], F32) for i in range(2)]
        sC = T("sC", [128, 2], F32)
        Irep = T("Irep", [128, 512], BF16)
        smB = T("smB", [128, 8], F32)
        smu = T("smu", [128, 4], U32)
        AR = st.enter_context(nc.sbuf_tensor("AR", [128, 12288], F32))

        def AV(off_kb, kb, dt, pattern=None, **kw):
            a = AR[:, off_kb * 256:(off_kb + kb) * 256]
            if dt != F32:
                a = a.bitcast(dt)
            if pattern:
                a = a.rearrange(pattern, **kw)
            return Tl(a, [("AR", g) for g in range(off_kb, off_kb + kb)])

        xt = [AV(0, 8, BF16, "p (k n) -> p k n", k=8), AV(8, 8, BF16, "p (k n) -> p k n", k=8)]
        posi = AV(16, 2, I32)
        ang = AV(18, 2, F32)
        kint = AV(20, 2, I32)
        r1t = AV(22, 2, F32)
        cost = AV(24, 2, F32)
        sint = AV(26, 2, F32)
        t1t = AV(28, 2, F32)
        t2t = AV(30, 2, F32)
        aqT = AV(32, 4, BF16, "p (c n) -> p c n", c=4)
        akO = AV(36, 2, BF16, "p (c n) -> p c n", c=2)
        akP = AV(38, 2, BF16, "p (c n) -> p c n", c=2)
        agT = T("agT", [128, 4, 512], BF16)
        avO = T("avO", [128, 4, 2, 129], BF16)
        avP = T("avP", [128, 4, 2, 129], BF16)
        Esw = [AV(44, 1, BF16), AV(45, 1, BF16)]
        Emsw = [AV(46, 1, BF16), AV(47, 1, BF16)]
        Esw1 = [T(f"Esw1{h}", [128, 512], BF16) for h in range(2)]
        Emsw1 = [T(f"Emsw1{h}", [128, 512], BF16) for h in range(2)]
        IS = [AV(2 * kc, 2, F32) for kc in range(16)]
        cand = T("cand", [128, 64, 16], F32)
        cand.keys = [("cand", c) for c in range(128)]
        rbuf2 = [[T(f"rbuf{p}{h}", [128, 512], BF16) for h in range(4)] for p in range(2)]
        Ee = [AV(40, 1, BF16), AV(41, 1, BF16)]
        Eo = [AV(42, 1, BF16), AV(43, 1, BF16)]
        Eme = [AV(44, 1, BF16), AV(45, 1, BF16)]
        Emo = [AV(46, 1, BF16), AV(47, 1, BF16)]

        ps = []
        for i in range(8):
            p = st.enter_context(nc.psum_tensor(f"ps{i}", [128, 512], F32))
            ps.append(Tl(p[:], [("ps", i)]))

        pg = Prog(nc)
        cs = cst_t.ap

        def dma(q, out_ap, in_ap, reads, writes):
            pg.op(q, lambda e: e.dma_start(out=out_ap, in_=in_ap), reads, writes, dma=True)

        def mmul(out_ap, lhsT, rhs, start, stop, reads, writes):
            pg.op("pe", lambda e: e.matmul(out_ap, lhsT, rhs, start=start, stop=stop), reads, writes)

        def act(out_ap, in_ap, func, reads, writes, bias=0.0, scale=1.0, accum=None):
            pg.op("act", lambda e: e.activation(out_ap, in_ap, func, bias=bias, scale=scale,
                                                accum_out=accum), reads, writes)

        def ts(eng, out_ap, in_ap, s1, s2, op0, op1, reads, writes, accum=None):
            if accum is None:
                pg.op(eng, lambda e: e.tensor_scalar(out_ap, in_ap, s1, s2, op0=op0, op1=op1), reads, writes)
            else:
                pg.op(eng, lambda e: e.tensor_scalar(out_ap, in_ap, s1, s2, op0=op0, op1=op1,
                                                     accum_out=accum), reads, writes)

        def tt(eng, out_ap, a, b, op, reads, writes):
            pg.op(eng, lambda e: e.tensor_tensor(out=out_ap, in0=a, in1=b, op=op), reads, writes)

        def stt(out_ap, in0, scalar, in1, op0, op1, reads, writes):
            pg.op("dve", lambda e: e.scalar_tensor_tensor(out=out_ap, in0=in0, scalar=scalar, in1=in1,
                                                          op0=op0, op1=op1), reads, writes)

        dma("sp", cst_t.ap, cst, [], [cst_t])
        dma("sp", bias_t.ap, biasT, [], [bias_t])
        dma("pool", wo_t.ap.rearrange("p k n -> p (k n)"), wo, [], [wo_t])
        dma("pool", bo_t.ap[0:1, :], bo, [], [bo_t])
        dma("sp", lng_t.ap, lng.to_broadcast([128, 1024]), [], [lng_t])
        dma("sp", lnb_t.ap, lnb.to_broadcast([128, 1024]), [], [lnb_t])
        dma("pool", wvg_t.ap.rearrange("p k n -> p (k n)"), wvg, [], [wvg_t])
        dma("pool", wvo_t.ap.rearrange("p k n -> p (k n)"), wvo, [], [wvo_t])
        dma("sp", bvg_t.ap, bvg.to_broadcast([128, 64]), [], [bvg_t])
        dma("sp", bvo_t.ap, bvo.to_broadcast([128, 132]), [], [bvo_t])
        dma("sp", snk.ap[64:65, :], sinks[0:1, 0:4], [], [snk])
        dma("sp", snk.ap[32:33, :], sinks[0:1, 4:8], [], [snk])
        pg.op("pool", lambda e: e.memset(ident.ap, 0.0), [], [ident])
        pg.op("pool", lambda e: e.affine_select(out=ident.ap, in_=ident.ap, pattern=[[-1, 128]],
                                                compare_op=ALU.not_equal, fill=1.0, base=0,
                                                channel_multiplier=1), [ident], [ident])
        pg.op("pool", lambda e: e.memset(ones_t.ap, 1.0), [], [ones_t])
        for q in range(4):
            pg.op("pool", (lambda o: lambda e: e.tensor_copy(o, ident.ap))(Irep.ap[:, q * 128:(q + 1) * 128]), [ident], [Irep])
        pg.op("pool", lambda e: e.memset(negbig.ap, -1.0e29), [], [negbig])
        pg.op("dve", lambda e: e.tensor_copy(mskb.ap, cs[:, C_SWM:C_SWM + 512]), [cst_t], [mskb])
        pg.op("pool", lambda e: e.memset(neghalf.ap, -0.5), [], [neghalf])
        for _t in bqTs:
            pg.op("pool", (lambda f: lambda e: e.memset(f, 0.0))(_t.ap.rearrange("p h n -> p (h n)")), [], [_t])
        pg.op("pool", lambda e: e.memset(iqT.ap.rearrange("p h n -> p (h n)"), 0.0), [], [iqT])
        pg.op("pool", lambda e: e.memset(Sel.ap, 0.0), [], [Sel])
        pg.op("pool", lambda e: e.memset(Sel.ap[64:65, 0:64], 1.0), [Sel], [Sel])
        pg.op("pool", lambda e: e.memset(Sel.ap[32:33, 64:128], 1.0), [Sel], [Sel])
        pg.op("pool", lambda e: e.memset(Rt.ap, 0.0), [], [Rt])
        pg.op("pool", lambda e: e.memset(onesf.ap, 1.0), [], [onesf])
        for vt in (Vb, avO, avP):
            flat = vt.ap.rearrange("p a b -> p (a b)") if len(vt.ap.shape) == 3 else vt.ap.rearrange("p a g b -> p (a g b)")
            pg.op("pool", (lambda f: lambda e: e.memset(f, 0.0))(flat), [], [vt])
            v3 = vt.ap if len(vt.ap.shape) == 3 else vt.ap.rearrange("p a g b -> p (a g) b")
            pg.op("pool", (lambda f: lambda e: e.memset(f, 1.0))(v3[:, :, 32:33]), [], [vt])
            pg.op("pool", (lambda f: lambda e: e.memset(f, 1.0))(v3[:, :, 128:129]), [], [vt])
        act(snk.ap[64:65, :], snk.ap[64:65, :], AF.Exp, [snk], [snk])
        act(snk.ap[32:33, :], snk.ap[32:33, :], AF.Exp, [snk], [snk])

        wcount = [0]

        wbf_ready = [False]

        def loadW(cid):
            w = Wb[wcount[0] % 4]
            wcount[0] += 1
            if wbf_ready[0]:
                dma("sp", w.ap.rearrange("p k n -> p (k n)"), wbf[cid], [("wbf", cid)], [w])
            else:
                dma("pool", w.ap.rearrange("p k n -> p (k n)"), wT[cid], [], [w])
            return w

        def cast_weights():
            for cid in range(2, 34):
                dma("pool", wbf[cid], wT[cid], [], [("wbf", cid)])
            wbf_ready[0] = True

        def load_x(buf, src, g):
            dma("pool", buf.ap, src[:, :, g * 512:(g + 1) * 512].rearrange("k p n -> p k n"), [], [buf])

        TSETS = [dict(posi=posi, ang=ang, kint=kint, r1t=r1t, cost=cost, sint=sint),
                 dict(posi=AV(32, 2, I32), ang=AV(34, 2, F32), kint=AV(36, 2, I32), r1t=AV(38, 2, F32),
                      cost=AV(40, 2, F32), sint=AV(42, 2, F32))]

        def tables(pos_src, g, tset=0):
            T_ = TSETS[tset]
            posi, ang, kint, r1t, cost, sint = (T_["posi"], T_["ang"], T_["kint"], T_["r1t"], T_["cost"], T_["sint"])
            dma("sp", posi.ap, pos_src[0:1, g * 512:(g + 1) * 512].to_broadcast([128, 512]), [], [posi])
            ts("dve", ang.ap, posi.ap, cs[:, C_INV:C_INV + 1], None, ALU.mult, ALU.bypass, [posi, cst_t], [ang])
            ts("dve", kint.ap, ang.ap, 1.0 / TWO_PI, None, ALU.mult, ALU.bypass, [ang], [kint])
            stt(r1t.ap, kint.ap, -C1, ang.ap, ALU.mult, ALU.add, [kint, ang], [r1t])
            stt(r1t.ap, kint.ap, -C2, r1t.ap, ALU.mult, ALU.add, [kint, r1t], [r1t])
            ts("dve", r1t.ap, r1t.ap, -PI32, PI32, ALU.max, ALU.min, [r1t], [r1t])
            act(sint.ap, r1t.ap, AF.Sin, [r1t, cst_t], [sint], scale=cs[:, C_SGN:C_SGN + 1])
            ts("dve", kint.ap, ang.ap, 1.0 / TWO_PI, 0.25, ALU.mult, ALU.add, [ang], [kint])
            stt(r1t.ap, kint.ap, -C1, ang.ap, ALU.mult, ALU.add, [kint, ang], [r1t])
            stt(r1t.ap, kint.ap, -C2, r1t.ap, ALU.mult, ALU.add, [kint, r1t], [r1t])
            ts("dve", r1t.ap, r1t.ap, -1.5 * PI32, 0.5 * PI32, ALU.max, ALU.min, [r1t], [r1t])
            act(cost.ap, r1t.ap, AF.Sin, [r1t, cst_t], [cost], bias=cs[:, C_HPI:C_HPI + 1], scale=SHR)

        pbank = [0]

        def projT(xtile, w, bank):
            for k in range(8):
                mmul(ps[bank].ap, w.ap[:, k, :], xtile.ap[:, k, :], k == 0, k == 7, [w, xtile], [ps[bank]])

        t1alt = [t1t, AV(44, 2, F32)]
        t2alt = [t2t, AV(46, 2, F32)]

        PAIRS = [(0, 1), (2, 7)]

        def run_jobs(xtile, jobs):
            for _ in run_jobs_gen(xtile, jobs):
                pass

        def run_jobs_gen(xtile, jobs):
            def preload(job):
                return [loadW(job[1]), loadW(job[2])] if job[0] == "rope" else [loadW(job[1])]
            nxt = preload(jobs[0])
            for n, job in enumerate(jobs):
                cur = nxt
                if n + 1 < len(jobs):
                    nxt = preload(jobs[n + 1])
                b0, b1 = PAIRS[pbank[0] % 2]
                pbank[0] += 1
                if job[0] == "rope":
                    _, cid, cidp, out_tl, out_ap = job
                    ta, tb = t1alt[n % 2], t2alt[n % 2]
                    projT(xtile, cur[0], b0)
                    projT(xtile, cur[1], b1)
                    stt(ta.ap, ps[b0].ap, bias_t.ap[:, cid:cid + 1], cost.ap, ALU.add, ALU.mult,
                        [ps[b0], bias_t, cost], [ta])
                    stt(tb.ap, ps[b1].ap, bias_t.ap[:, cidp:cidp + 1], sint.ap, ALU.add, ALU.mult,
                        [ps[b1], bias_t, sint], [tb])
                    if isinstance(out_ap, tuple):
                        tt("pool", out_ap[0], ta.ap[0:64, :], tb.ap[0:64, :], ALU.add, [ta, tb], [out_tl])
                        tt("pool", out_ap[1], ta.ap[64:128, :], tb.ap[64:128, :], ALU.add, [ta, tb], [out_tl])
                    else:
                        tt("pool", out_ap, ta.ap, tb.ap, ALU.add, [ta, tb], [out_tl])
                else:
                    _, cid, out_tl, out_ap = job
                    projT(xtile, cur[0], b0)
                    act(out_ap, ps[b0].ap, AF.Silu, [ps[b0], bias_t], [out_tl], bias=bias_t.ap[:, cid:cid + 1])
                yield

        vbank = [0]

        def vproj(xtile, tt_, wtile, ncol):
            bank = (1, 7)[vbank[0] % 2]
            vbank[0] += 1
            for k in range(8):
                mmul(ps[bank].ap[:, 0:ncol], xtile.ap[:, k, tt_ * 128:(tt_ + 1) * 128], wtile.ap[:, k, :],
                     k == 0, k == 7, [xtile, wtile], [ps[bank]])
            return bank

        WG = [None] * 2
        load_x(xt[0], xg, 0)
        for c in range(2):
            WG[c] = loadW(c)
        for g in range(16):
            xb = xt[g % 2]
            if g + 1 < 16:
                load_x(xt[(g + 1) % 2], xg, g + 1)
            tables(posg, g, g % 2)
            gcost, gsint = TSETS[g % 2]["cost"], TSETS[g % 2]["sint"]
            b0 = pbank[0] % 2 * 2
            pbank[0] += 1
            projT(xb, WG[0], b0)
            projT(xb, WG[1], b0 + 1)
            ta, tb = t1alt[g % 2], t2alt[g % 2]
            stt(ta.ap, ps[b0].ap, bias_t.ap[:, 0:1], gcost.ap, ALU.add, ALU.mult, [ps[b0], bias_t, gcost], [ta])
            stt(tb.ap, ps[b0 + 1].ap, bias_t.ap[:, 1:2], gsint.ap, ALU.add, ALU.mult, [ps[b0 + 1], bias_t, gsint], [tb])
            tt("pool", KT.ap[:, g * 512:(g + 1) * 512], ta.ap, tb.ap, ALU.add, [ta, tb], [KT])
            for t4 in range(4):
                bank = vproj(xb, t4, wvg_t, 64)
                tt("dve", Vb.ap[:, g * 4 + t4, 64:128], ps[bank].ap[:, 0:64], bvg_t.ap, ALU.add,
                   [ps[bank], bvg_t], [Vb])

        def normalize(bank_e, bank_o, gate_tl, gate_ap, out_tl, out_ap, with_sink, bb0):
            pe_, po_ = ps[bank_e], ps[bank_o]
            if with_sink:
                tt("dve", Rt.ap[64:65, :].rearrange("p (h t) -> p h t", h=4),
                   pe_.ap[64:65, :].rearrange("p (h t) -> p h t", h=4),
                   snk.ap[64:65, :].unsqueeze(2).to_broadcast([1, 4, 128]), ALU.add, [pe_, snk], [Rt])
                tt("dve", Rt.ap[32:33, :].rearrange("p (h t) -> p h t", h=4),
                   po_.ap[32:33, :].rearrange("p (h t) -> p h t", h=4),
                   snk.ap[32:33, :].unsqueeze(2).to_broadcast([1, 4, 128]), ALU.add, [po_, snk], [Rt])
            else:
                act(Rt.ap[64:65, :], pe_.ap[64:65, :], AF.Identity, [pe_], [Rt])
                act(Rt.ap[32:33, :], po_.ap[32:33, :], AF.Identity, [po_], [Rt])
            mmul(ps[bb0].ap, Sel.ap, Rt.ap, True, True, [Sel, Rt], [ps[bb0]])
            pg.op("dve", lambda e: e.reciprocal(Rbc.ap, ps[bb0].ap), [ps[bb0]], [Rbc])
            tt("dve", Rbc.ap[0:64, :], pe_.ap[0:64, :], Rbc.ap[0:64, :], ALU.mult, [pe_, Rbc], [Rbc])
            tt("dve", Rbc.ap[64:128, :], po_.ap[64:128, :], Rbc.ap[64:128, :], ALU.mult, [po_, Rbc], [Rbc])
            tt("pool", out_ap, Rbc.ap.rearrange("p (h t) -> p h t", h=4), gate_ap, ALU.mult,
               [Rbc, gate_tl], [out_tl])

        def stageA(i, bl):
            NC_ = i + 1
            CS = 8 * NC_
            par = i % 2
            sA, dg = smA[par], Dg[par]
            tcols = slice(bl * 128, (bl + 1) * 128)
            wv = wtok.ap[:, bl, :]
            absw, sgn = sA.ap[:, 0:4], sA.ap[:, 4:8]
            ts("dve", absw, wv, -1.0, None, ALU.mult, ALU.bypass, [wtok], [sA])
            tt("dve", absw, absw, wv, ALU.max, [wtok, sA], [sA])
            ts("dve", sgn, wv, 0.0, 2.0, ALU.is_ge, ALU.mult, [wtok], [sA])
            ts("dve", sgn, sgn, -1.0, None, ALU.add, ALU.bypass, [sA], [sA])
            for h in range(4):
                ts("dve", dg.ap[:, h, :], ident.ap, sA.ap[:, 4 + h:5 + h], None, ALU.mult, ALU.bypass,
                   [ident, sA], [dg])
            yield
            for kc in range(NC_):
                pb = [(0, 1, 2, 7)[(h + kc) % 4] for h in range(4)]
                rbuf = rbuf2[kc % 2]
                for h in range(4):
                    mmul(ps[pb[h]].ap, iqT.ap[:, h, tcols], KT.ap[:, kc * 512:(kc + 1) * 512],
                         True, True, [iqT, KT], [ps[pb[h]]])
                yield
                for h in range(4):
                    if h % 2 == 0:
                        act(rbuf[h].ap, ps[pb[h]].ap, AF.Relu, [ps[pb[h]], sA], [rbuf[h]], scale=sA.ap[:, h:h + 1])
                    else:
                        ts("dve", rbuf[h].ap, ps[pb[h]].ap, sA.ap[:, h:h + 1], 0.0, ALU.mult, ALU.max,
                           [ps[pb[h]], sA], [rbuf[h]])
                yield
                ab = pb[0]
                for h in range(4):
                    mmul(ps[ab].ap, dg.ap[:, h, :], rbuf[h].ap, h == 0, h == 3, [dg, rbuf[h]], [ps[ab]])
                stt(IS[kc].ap, ps[ab].ap, float(-1e-30 * 512 * kc), cs[:, C_EPS:C_EPS + 512], ALU.add, ALU.add,
                    [ps[ab], cst_t], [IS[kc]])
                if kc == NC_ - 1:
                    tt("dve", IS[kc].ap, IS[kc].ap, cs[:, C_CNEG:C_CNEG + 512], ALU.min, [IS[kc], cst_t], [IS[kc]])
                yield
            isflat = AR[:, 0:512 * NC_]
            iskeys = [IS[k] for k in range(NC_)]
            cflat = cand.ap.rearrange("p c e -> p (c e)")
            nlo, hi, wdt, nmid, cnt, lo = (sA.ap[:, 8:9], sA.ap[:, 9:10], sA.ap[:, 10:11], sA.ap[:, 11:12],
                                            sA.ap[:, 12:13], sA.ap[:, 13:14])
            if NC_ <= 2:
                nbis = NBIS + 4
                cflat = isflat
                ckeys = iskeys
                jk = junkA.ap[:, 0:512 * NC_]
                pg.op("dve", lambda e: e.tensor_reduce(out=hi, in_=isflat, op=ALU.max, axis=mybir.AxisListType.X), iskeys, [sA])
                ts("dve", cand.ap.rearrange("p c e -> p (c e)")[:, 0:512 * NC_], isflat, -1.0e29, 3.0e30, ALU.is_lt, ALU.mult,
                   iskeys, [cand])
                tt("dve", cand.ap.rearrange("p c e -> p (c e)")[:, 0:512 * NC_], cand.ap.rearrange("p c e -> p (c e)")[:, 0:512 * NC_],
                   isflat, ALU.add, iskeys + [cand], [cand])
                pg.op("dve", lambda e: e.tensor_reduce(out=lo, in_=cand.ap.rearrange("p c e -> p (c e)")[:, 0:512 * NC_],
                                                       op=ALU.min, axis=mybir.AxisListType.X), [cand], [sA])
            else:
                nbis = NBIS
                ckeys = [cand]
                jk = junkA.ap
                cand8 = cflat.rearrange("p (c e) -> p c e", e=8)
                CS2 = CS // 2
                for c in range(128):
                    src = isflat[:, c * CS2:(c + 1) * CS2]
                    pg.op("dve", (lambda o, s_: lambda e: e.max(out=o, in_=s_))(cand8[:, c, :], src), iskeys, [("cand", c)])
                    if c % 4 == 3:
                        yield
                chi = cand8[:, :, 1:2].rearrange("p c e -> p (c e)")
                clo = cand8[:, :, 3:4].rearrange("p c e -> p (c e)")
                tv = sA.ap[:, 32:160]
                pg.op("dve", lambda e: e.tensor_reduce(out=hi, in_=chi, op=ALU.max, axis=mybir.AxisListType.X), [cand], [sA])
                ts("dve", tv, clo, -1.0e29, 3.0e30, ALU.is_lt, ALU.mult, [cand], [sA])
                tt("dve", tv, tv, clo, ALU.add, [cand, sA], [sA])
                pg.op("dve", lambda e: e.tensor_reduce(out=lo, in_=tv, op=ALU.min, axis=mybir.AxisListType.X), [sA], [sA])
            tt("dve", wdt, hi, lo, ALU.subtract, [sA], [sA])
            yield
            mid = nmid
            for it in range(nbis):
                stt(mid, wdt, float(0.5 ** (it + 1)), lo, ALU.mult, ALU.add, [sA], [sA])
                ts("dve", jk, cflat, mid, None, ALU.is_ge, ALU.add, ckeys + [sA, junkA], [junkA, sA], accum=cnt)
                ts("dve", smu.ap[:, 0:1], cnt, 255.5, None, ALU.is_gt, ALU.bypass, [sA], [smu])
                pg.op("dve", lambda e: e.copy_predicated(lo, smu.ap[:, 0:1], mid), [sA, smu], [sA])
                yield
            for kc in range(NC_):
                ts("dve", MA[kc].ap, IS[kc].ap, lo, -32768.0, ALU.is_lt, ALU.mult, [IS[kc], sA], [MA[kc]])
            yield

        def unitsA(i):
            return (1 + 3 * (i + 1) + 1 + NBIS + 4 + 1) if i <= 1 else (1 + 3 * (i + 1) + 32 + 1 + NBIS + 1)

        def stageB(i, bl):
            NC_ = i + 1
            NK = 4 * NC_
            tcols = slice(bl * 128, (bl + 1) * 128)
            dma("sp", zt.ap, xtok[i * 128:(i + 1) * 128, :], [], [zt])
            bqT = bqTs[(i // 4) % 2]
            bgT = bgTs[(i // 4) % 2]
            bq2 = bqT.ap.rearrange("p (m two) n -> p two m n", two=2)
            Ebuf = [Ee[0], Ee[1], Eo[0], Eo[1]]

            for half in range(2):
                def qk(kt):
                    ks = slice(kt * 128, (kt + 1) * 128)
                    bank = 3 + kt % 2
                    mmul(ps[bank].ap, KT.ap[:, ks], bq2[:, half][:, :, tcols], True, False, [KT, bqT], [ps[bank]])
                    mmul(ps[bank].ap, MA[kt // 4].ap[:, (kt % 4) * 128:(kt % 4 + 1) * 128], Irep.ap, False, True,
                         [MA[kt // 4], Irep], [ps[bank]])

                def pv(kt):
                    em = Ebuf[kt % 4]
                    if half == 0:
                        vflat = Vb.ap.rearrange("p a b -> p (a b)")
                        mmul(ps[5].ap, vflat[:, kt * 129 + 64:kt * 129 + 192], em.ap, kt == 0, kt == NK - 1, [Vb, em], [ps[5]])
                    else:
                        mmul(ps[6].ap, Vb.ap[:, kt, 0:128], em.ap, kt == 0, kt == NK - 1, [Vb, em], [ps[6]])

                qk(0)
                for kt in range(NK):
                    if kt + 1 < NK:
                        qk(kt + 1)
                    if kt >= 1:
                        pv(kt - 1)
                    eb = Ebuf[kt % 4]
                    act(eb.ap, ps[3 + kt % 2].ap, AF.Exp, [ps[3 + kt % 2]], [eb], scale=0.125)
                    yield
                pv(NK - 1)
                yield
            normalize(5, 6, bgT, bgT.ap[:, :, tcols], mixB, mixB.ap, False, 3)
            yield
            for hf in range(2):
                bank = 3 + hf
                cols = slice(hf * 512, (hf + 1) * 512)
                for c in range(8):
                    lhsT = mixA.ap[:, c, tcols] if c < 4 else mixB.ap[:, c - 4, :]
                    mmul(ps[bank].ap, lhsT, wo_t.ap[:, c, cols], c == 0, False, [mixA, mixB, wo_t], [ps[bank]])
                mmul(ps[bank].ap, ones_t.ap[0:1, :], bo_t.ap[0:1, cols], False, True, [ones_t, bo_t], [ps[bank]])
                stt(zt.ap[:, cols], zt.ap[:, cols], ALPHA, ps[bank].ap, ALU.mult, ALU.add, [zt, ps[bank]], [zt])
            yield
            ssum, ssq, mean, var = smB.ap[:, 0:1], smB.ap[:, 1:2], smB.ap[:, 2:3], smB.ap[:, 3:4]
            pg.op("pool", lambda e: e.memset(smB.ap[:, 0:2], 0.0), [], [smB])
            act(junkA.ap, zt.ap, AF.Identity, [zt, junkA], [junkA, smB], accum=ssum)
            act(junkA.ap, zt.ap, AF.Square, [zt, junkA], [junkA, smB], accum=ssq)
            ts("dve", mean, ssum, 1.0 / 1024, None, ALU.mult, ALU.bypass, [smB], [smB])
            stt(var, mean, -1.0, mean, ALU.mult, ALU.mult, [smB], [smB])
            stt(var, ssq, 1.0 / 1024, var, ALU.mult, ALU.add, [smB], [smB])
            ts("dve", var, var, LN_EPS, None, ALU.add, ALU.bypass, [smB], [smB])
            pg.op("pool", lambda e: e.tensor_tensor(out=var, in0=var, in1=neghalf.ap, op=ALU.pow), [smB, neghalf], [smB])
            ts("dve", zt.ap, zt.ap, mean, var, ALU.subtract, ALU.mult, [zt, smB], [zt])
            tt("pool", zt.ap, zt.ap, lng_t.ap, ALU.mult, [zt, lng_t], [zt])
            tt("pool", zt.ap, zt.ap, lnb_t.ap, ALU.add, [zt, lnb_t], [zt])
            dma("sp", out[i * 128:(i + 1) * 128, :], zt.ap, [zt], [])
            yield

        def unitsB(i):
            return 8 * (i + 1) + 6

        def drain(g):
            for _ in g:
                pass

        def merge(ga, na, gb, nb):
            da = db = 0
            alive_a, alive_b = ga is not None, True
            while alive_a or alive_b:
                take_a = alive_a and (not alive_b or da * nb <= db * na)
                if take_a:
                    try:
                        next(ga)
                        da += 1
                    except StopIteration:
                        alive_a = False
                else:
                    try:
                        next(gb)
                        db += 1
                    except StopIteration:
                        alive_b = False

        def opass_gen(og):
            bqT, bgT = bqTs[og % 2], bgTs[og % 2]
            xb = xt[0]
            load_x(xb, xo, og)
            load_x(xt[1], xp, og)
            tables(poso, og)
            yield
            jobs = []
            jobs += [("rope", 10 + m, 12 + m, iqT, (iqT.ap[64:128, 2 * m, :], iqT.ap[64:128, 2 * m + 1, :])) for m in range(2)]
            jobs += [("rope", 14 + m, 18 + m, aqT, aqT.ap[:, m, :]) for m in range(4)]
            jobs += [("rope", 22 + m, 24 + m, akO, akO.ap[:, m, :]) for m in range(2)]
            jobs += [("rope", 2 + m, 6 + m, bqT, (bqT.ap[0:64, 2 * m, :], bqT.ap[0:64, 2 * m + 1, :])) for m in range(4)]
            jobs += [("gate", 26 + m, agT, agT.ap[:, m, :]) for m in range(4)]
            jobs += [("gate", 30 + m, bgT, bgT.ap[:, m, :]) for m in range(4)]
            for _ in run_jobs_gen(xb, jobs):
                yield
            for t4 in range(4):
                bank = vproj(xb, t4, wvo_t, 132)
                tt("dve", avO.ap[:, t4, :, 64:128], ps[bank].ap[:, 0:128].rearrange("p (g d) -> p g d", g=2),
                   bvo_t.ap[:, 0:128].rearrange("p (g d) -> p g d", g=2), ALU.add, [ps[bank], bvo_t], [avO])
                tt("dve", wtok.ap[:, t4, :], ps[bank].ap[:, 128:132], bvo_t.ap[:, 128:132], ALU.add,
                   [ps[bank], bvo_t], [wtok])
                ts("dve", wtok.ap[:, t4, :], wtok.ap[:, t4, :], 1.0 / 16.0, None, ALU.mult, ALU.bypass, [wtok], [wtok])
                yield
            xb = xt[1]
            tables(posp, og)
            yield
            for _ in run_jobs_gen(xb, [("rope", 22 + m, 24 + m, akP, akP.ap[:, m, :]) for m in range(2)]):
                yield
            for t4 in range(4):
                bank = vproj(xb, t4, Tl(wvo_t.ap[:, :, 0:128], wvo_t.keys), 128)
                tt("dve", avP.ap[:, t4, :, 64:128], ps[bank].ap[:, 0:128].rearrange("p (g d) -> p g d", g=2),
                   bvo_t.ap[:, 0:128].rearrange("p (g d) -> p g d", g=2), ALU.add, [ps[bank], bvo_t], [avP])
                yield

        NOPASS = 1 + 20 + 4 + 1 + 2 + 4

        for og in range(4):
            def swa_gen(og=og, piped=(og != 0)):
                def banks(bl, g):
                    if not piped:
                        return (3, 4), (5, 6), 3, (Esw, Emsw)
                    sb = (3, 4) if g == 0 else (2, 7)
                    ob = (5, 6) if bl % 2 == 0 else (0, 1)
                    return sb, ob, (7 if bl % 2 == 0 else 2), ((Esw, Emsw) if g == 0 else (Esw1, Emsw1))

                def s1(bl, g):
                    sb, _, _, _ = banks(bl, g)
                    tcols = slice(bl * 128, (bl + 1) * 128)
                    for half in range(2):
                        rows = slice(64 * half, 64 * half + 64)
                        for kt2, aksrc in enumerate((akP, akO)):
                            mmul(ps[sb[half]].ap[:, kt2 * 256:(kt2 + 1) * 256],
                                 aksrc.ap[rows, g, tcols], aqT.ap[rows, 2 * g:2 * g + 2, tcols], True, True,
                                 [aksrc, aqT], [ps[sb[half]]])

                def s2(bl, g):
                    sb, _, _, (E_, Em_) = banks(bl, g)
                    mcol = 256 if (og == 0 and bl == 0) else 0
                    for half in range(2):
                        act(E_[half].ap, ps[sb[half]].ap, AF.Exp, [ps[sb[half]]], [E_[half]], scale=0.125)
                        e4 = E_[half].ap.rearrange("p (k h t) -> p k h t", k=2, h=2)
                        o4 = Em_[half].ap.rearrange("p (k h t) -> p k h t", k=2, h=2)
                        for kt2 in range(2):
                            mk = bc_mid(mskb.ap[:, mcol + kt2 * 128:mcol + kt2 * 128 + 128], 2)
                            tt("dve", o4[:, kt2], e4[:, kt2], mk, ALU.mult, [E_[half], mskb], [Em_[half]])

                def s3(bl, g):
                    _, ob, _, (E_, Em_) = banks(bl, g)
                    for kt2, avsrc in enumerate((avP, avO)):
                        mmul(ps[ob[0]].ap[0:65, g * 256:(g + 1) * 256], avsrc.ap[:, bl, g, 64:129],
                             Em_[0].ap[:, kt2 * 256:(kt2 + 1) * 256],
                             kt2 == 0, kt2 == 1, [avsrc, Em_[0]], [ps[ob[0]]])
                        mmul(ps[ob[1]].ap[:, g * 256:(g + 1) * 256], avsrc.ap[:, bl, g, 0:128],
                             Em_[1].ap[:, kt2 * 256:(kt2 + 1) * 256],
                             kt2 == 0, kt2 == 1, [avsrc, Em_[1]], [ps[ob[1]]])

                def s4(bl):
                    _, ob, bb0, _ = banks(bl, 0)
                    tcols = slice(bl * 128, (bl + 1) * 128)
                    normalize(ob[0], ob[1], agT, agT.ap[:, :, tcols], mixA, mixA.ap[:, :, tcols], True, bb0)

                if not piped:
                    for bl in range(4):
                        for g in range(2):
                            s1(bl, g)
                            s2(bl, g)
                            s3(bl, g)
                            yield
                        s4(bl)
                        yield
                else:
                    s1(0, 0)
                    s1(0, 1)
                    s2(0, 0)
                    s2(0, 1)
                    for bl in range(4):
                        if bl + 1 < 4:
                            s1(bl + 1, 0)
                            s1(bl + 1, 1)
                        s3(bl, 0)
                        s3(bl, 1)
                        if bl + 1 < 4:
                            s2(bl + 1, 0)
                            s2(bl + 1, 1)
                        s4(bl)
                        yield

            if og == 0:
                drain(opass_gen(0))
                cast_weights()
                drain(stageA(0, 0))
                drain(swa_gen(piped=True))
            else:
                drain(swa_gen())
            for bl in range(4):
                i = og * 4 + bl
                if bl < 3:
                    merge(stageA(i + 1, bl + 1), unitsA(i + 1), stageB(i, bl), unitsB(i))
                elif og < 3:
                    drain(opass_gen(og + 1))
                    merge(stageA(i + 1, 0), unitsA(i + 1), stageB(i, bl), unitsB(i))
                else:
                    drain(stageB(i, bl))

        counts = pg.emit(st)
    return nc, counts


_CACHE = {}


def _chunks():
    AQ, AK, AV_, AG, BQ, BK, BV, BG, IQ, IK, IW = 0, 512, 640, 768, 1280, 1792, 1856, 1920, 2432, 2688, 2752
    perm = (np.arange(64) + 32) % 64
    r64 = np.arange(64)

    def heads(base, hl, p):
        return np.concatenate([base + 64 * h + (perm if p else r64) for h in hl])

    ch = []
    ch += [np.concatenate([BK + r64, IK + r64]), np.concatenate([BK + perm, IK + perm])]
    ch += [heads(BQ, [2 * m, 2 * m + 1], False) for m in range(4)]
    ch += [heads(BQ, [2 * m, 2 * m + 1], True) for m in range(4)]
    ch += [heads(IQ, [2 * m, 2 * m + 1], False) for m in range(2)]
    ch += [heads(IQ, [2 * m, 2 * m + 1], True) for m in range(2)]
    ch += [heads(AQ, [2 * m, 2 * m + 1], False) for m in range(4)]
    ch += [heads(AQ, [2 * m, 2 * m + 1], True) for m in range(4)]
    ch += [heads(AK, [g, g], False) for g in range(2)]
    ch += [heads(AK, [g, g], True) for g in range(2)]
    ch += [AG + 128 * m + np.arange(128) for m in range(4)]
    ch += [BG + 128 * m + np.arange(128) for m in range(4)]
    return ch, (AV_, BV, IW)


def kernel(x, positions, w_in, b_in, swa_sinks, w_out, b_out, ln_gain, ln_bias):
    x = np.asarray(x, dtype=np.float32)
    positions = np.asarray(positions, dtype=np.int32)
    w = np.asarray(w_in, dtype=np.float32)[0]
    b = np.asarray(b_in, dtype=np.float32)[0]
    sk = np.asarray(swa_sinks, dtype=np.float32)[0]
    wo_ = np.asarray(w_out, dtype=np.float32)[0]
    bo_ = np.asarray(b_out, dtype=np.float32)
    lg = np.asarray(ln_gain, dtype=np.float32)
    lb = np.asarray(ln_bias, dtype=np.float32)

    if "nc" not in _CACHE:
        _CACHE["nc"] = build()[0]
    nc = _CACHE["nc"]

    ch, (AV_, BV, IW) = _chunks()
    wT = np.stack([np.ascontiguousarray(w[:, c].reshape(8, 128, 128).transpose(1, 0, 2)).reshape(128, 1024) for c in ch])
    biasT = np.ascontiguousarray(np.stack([b[c] for c in ch], axis=1))
    cg = BV + np.arange(64)
    co = np.concatenate([AV_ + np.arange(128), IW + np.arange(4)])
    wvg = np.ascontiguousarray(w[:, cg].reshape(8, 128, 64).transpose(1, 0, 2)).reshape(128, 512)
    wvo = np.ascontiguousarray(w[:, co].reshape(8, 128, 132).transpose(1, 0, 2)).reshape(128, 8 * 132)
    bvg = np.ascontiguousarray(b[cg][None, :])
    bvo = np.ascontiguousarray(b[co][None, :])
    wo_h = np.ascontiguousarray(wo_.reshape(8, 128, 1024).transpose(1, 0, 2)).reshape(128, 8192)
    sinks = np.ascontiguousarray(np.concatenate([sk[0::2], sk[1::2]])[None, :])

    inv = (np.float32(10000.0) ** (-np.arange(0, 64, 2, dtype=np.float32) / np.float32(64))).astype(np.float32)
    pidx = np.arange(128) % 64
    cst_base = np.zeros((128, C_END), np.float32)
    cst_base[:, C_INV] = inv[pidx % 32]
    cst_base[:, C_SGN] = np.where(pidx < 32, -1.0, 1.0) * SHR
    cst_base[:, C_HPI] = (math.pi / 2) * SHR
    cst_base[:, C_EPS:C_EPS + 512] = (-1e-30 * np.arange(512, dtype=np.float64)).astype(np.float32)[None, :]
    s_i = np.arange(128)[:, None]
    t_i = np.arange(128)[None, :]
    swm = np.concatenate([(s_i > t_i).astype(np.float32), (s_i <= t_i).astype(np.float32)], axis=1)

    in_maps = []
    own_idx = []
    for c in range(8):
        bb, j = c // 4, c % 4
        blocks = 4 * np.arange(16) + j
        oi = (blocks[:, None] * 128 + np.arange(128)[None, :]).reshape(-1)
        pb = np.maximum(blocks - 1, 0)
        pi = (pb[:, None] * 128 + np.arange(128)[None, :]).reshape(-1)
        own_idx.append(oi)
        xT = np.ascontiguousarray(x[bb].T).reshape(8, 128, S)
        cst = cst_base.copy()
        cn = np.full((128, 512), -1e30, np.float32)
        for o in range(4):
            if o < j:
                cn[:, o * 128:(o + 1) * 128] = 1e30
            elif o == j:
                cn[:, o * 128:(o + 1) * 128] = np.where(s_i.T <= t_i.T, 1e30, -1e30)
        cst[:, C_CNEG:C_CNEG + 512] = cn
        cst[:, C_SWM:C_SWM + 256] = swm
        swf = swm.copy()
        if j == 0:
            swf[:, 0:128] = 0.0
        cst[:, C_SWMF:C_SWMF + 256] = swf
        in_maps.append({
            "xg": xT,
            "xo": np.ascontiguousarray(xT[:, :, oi]),
            "xp": np.ascontiguousarray(xT[:, :, pi]),
            "xtok": np.ascontiguousarray(x[bb][oi]),
            "posg": np.ascontiguousarray(positions[bb][None, :]),
            "poso": np.ascontiguousarray(positions[bb][oi][None, :]),
            "posp": np.ascontiguousarray(positions[bb][pi][None, :]),
            "wT": wT, "biasT": biasT, "wvg": wvg, "wvo": wvo, "bvg": bvg, "bvo": bvo,
            "wo": wo_h, "bo": bo_, "lng": lg, "lnb": lb, "sinks": sinks, "cst": cst,
        })
    res = run_bass_kernel_spmd(nc, in_maps, core_ids=list(range(8)))
    outp = np.empty((2, S, 1024), np.float32)
    for c in range(8):
        outp[c // 4, own_idx[c]] = res.results[c]["out"]
    return outp
```

```python
import math
from contextlib import ExitStack
import numpy as np
import concourse.bass as bass
import concourse.mybir as mybir
from concourse.bass_utils import run_bass_kernel_spmd

F32 = mybir.dt.float32
BF16 = mybir.dt.bfloat16
I32 = mybir.dt.int32
U32 = mybir.dt.uint32
ALU = mybir.AluOpType
AF = mybir.ActivationFunctionType
ENGS = ("pe", "act", "dve", "pool", "sp")

S = 8192
NOWN = 2048
ALPHA = 2.0 ** 0.25
LN_EPS = 1e-5
NBIS = 7
TWO_PI = 2.0 * math.pi
C1 = 6.28125
C2 = TWO_PI - 6.28125
SHR = 1.0 - 2e-6
PI32 = float(np.float32(math.pi))
C_INV, C_SGN, C_HPI, C_EPS, C_CNEG, C_SWM, C_SWMF, C_END = 0, 1, 2, 3, 515, 1027, 1283, 1539


class Tl:
    def __init__(self, ap, keys):
        self.ap = ap
        self.keys = list(keys)

    def __getitem__(self, idx):
        return self.ap[idx]


class Prog:
    def __init__(self, nc, n_dma_sems=8):
        self.nc = nc
        self.eng = {"pe": nc.tensor, "act": nc.scalar, "dve": nc.vector,
                    "pool": nc.gpsimd, "sp": nc.sync}
        self.ops = []
        self.last_w = {}
        self.readers = {}
        self.n_dma_sems = n_dma_sems

    @staticmethod
    def _keys(lst):
        out = []
        for k in lst:
            if isinstance(k, Tl):
                out.extend(k.keys)
            elif isinstance(k, (list, tuple)) and len(k) and isinstance(k[0], (list, tuple, Tl)):
                out.extend(Prog._keys(k))
            else:
                out.append(k)
        return out

    def op(self, eng, fn, reads=(), writes=(), dma=False):
        reads = self._keys(reads)
        writes = self._keys(writes)
        deps = set()
        for k in reads:
            if k in self.last_w:
                deps.add(self.last_w[k])
        for k in writes:
            if k in self.last_w:
                deps.add(self.last_w[k])
            for r in self.readers.get(k, ()):
                deps.add(r)
        oid = len(self.ops)
        self.ops.append([eng, fn, sorted(deps), dma])
        for k in reads:
            self.readers.setdefault(k, []).append(oid)
        for k in writes:
            self.last_w[k] = oid
            self.readers[k] = []
        return oid

    def emit(self, stack):
        nc = self.nc
        ops = self.ops
        n = len(ops)
        needed = [False] * n
        for rec in ops:
            eng, fn, deps, dma = rec
            latest = {}
            keep = []
            for d in deps:
                deng, _, _, ddma = ops[d]
                if ddma:
                    keep.append(d)
                    continue
                if deng == "pe" and eng == "pe":
                    continue
                if deng not in latest or d > latest[deng]:
                    latest[deng] = d
            keep.extend(latest.values())
            rec[2] = sorted(keep)
            for d in keep:
                needed[d] = True
        sems = {e: stack.enter_context(nc.semaphore("s_" + e)) for e in ENGS}
        dma_sems = {e: [stack.enter_context(nc.semaphore(f"d_{e}_{i}"))
                        for i in range(self.n_dma_sems)] for e in ("sp", "pool", "act")}
        sig_count = {e: 0 for e in ENGS}
        dma_count = {e: 0 for e in ENGS}
        handle = [None] * n
        waited = {e: {} for e in ENGS}
        last_dma = {}

        def wait(e, sem, val):
            key = id(sem)
            if waited[e].get(key, 0) >= val:
                return
            self.eng[e].wait_ge(sem, val)
            waited[e][key] = val

        for i, (eng, fn, deps, dma) in enumerate(ops):
            for d in deps:
                deng, _, _, ddma = ops[d]
                if deng == "pe" and eng == "pe" and not ddma:
                    continue
                h = handle[d]
                wait(eng, h[0], h[1])
            if dma:
                k = dma_count[eng]
                dma_count[eng] += 1
                sem = dma_sems[eng][k % self.n_dma_sems]
                rnd = k // self.n_dma_sems
                if rnd > 0:
                    wait(eng, sem, 16 * rnd)
                ins = fn(self.eng[eng])
                ins.then_inc(sem, 16)
                handle[i] = (sem, 16 * (rnd + 1))
                last_dma[(eng, k % self.n_dma_sems)] = handle[i]
            else:
                ins = fn(self.eng[eng])
                if needed[i]:
                    sig_count[eng] += 1
                    ins.then_inc(sems[eng], 1)
                    handle[i] = (sems[eng], sig_count[eng])
        for (eng, _), h in last_dma.items():
            wait(eng, h[0], h[1])
        return sig_count, dma_count


def bc_mid(ap2d, n):
    return ap2d.unsqueeze(1).to_broadcast([ap2d.shape[0], n, ap2d.shape[1]])


def build():
    nc = bass.Bass("TRN2", target_bir_lowering=False, dynamic_dma_scratch_size=4096)

    def D(name, shape, dt, kind="ExternalInput"):
        return nc.dram_tensor(name, shape, dt, kind=kind).ap()

    xg = D("xg", [8, 128, S], F32)
    xo = D("xo", [8, 128, NOWN], F32)
    xp = D("xp", [8, 128, NOWN], F32)
    xtok = D("xtok", [NOWN, 1024], F32)
    posg = D("posg", [1, S], I32)
    poso = D("poso", [1, NOWN], I32)
    posp = D("posp", [1, NOWN], I32)
    wT = D("wT", [34, 128, 1024], F32)
    biasT = D("biasT", [128, 34], F32)
    wvg = D("wvg", [128, 8 * 64], F32)
    wvo = D("wvo", [128, 8 * 132], F32)
    bvg = D("bvg", [1, 64], F32)
    bvo = D("bvo", [1, 132], F32)
    wo = D("wo", [128, 8 * 1024], F32)
    bo = D("bo", [1, 1024], F32)
    lng = D("lng", [1, 1024], F32)
    lnb = D("lnb", [1, 1024], F32)
    sinks = D("sinks", [1, 8], F32)
    cst = D("cst", [128, C_END], F32)
    out = D("out", [NOWN, 1024], F32, kind="ExternalOutput")
    wbf = D("wbf", [34, 128, 1024], BF16, kind="Internal")

    with ExitStack() as st:
        def T(name, shape, dt):
            t = st.enter_context(nc.sbuf_tensor(name, shape, dt))
            return Tl(t[:], [name])

        KT = T("KT", [128, S], BF16)
        Vb = T("Vb", [128, 64, 129], BF16)
        wo_t = T("wo_t", [128, 8, 1024], BF16)
        bo_t = T("bo_t", [128, 1024], BF16)
        lng_t = T("lng_t", [128, 1024], F32)
        lnb_t = T("lnb_t", [128, 1024], F32)
        cst_t = T("cst_t", [128, C_END], F32)
        bias_t = T("bias_t", [128, 34], F32)
        ident = T("ident", [128, 128], BF16)
        ones_t = T("ones_t", [128, 128], BF16)
        onesf = T("onesf", [128, 64], F32)
        wvg_t = T("wvg_t", [128, 8, 64], BF16)
        wvo_t = T("wvo_t", [128, 8, 132], BF16)
        bvg_t = T("bvg_t", [128, 64], F32)
        bvo_t = T("bvo_t", [128, 132], F32)
        snk = T("snk", [128, 4], F32)
        bqTs = [T(f"bqT{p}", [128, 8, 512], BF16) for p in range(2)]
        iqT = T("iqT", [128, 4, 512], BF16)
        bgTs = [T(f"bgT{p}", [128, 4, 512], BF16) for p in range(2)]
        mixA = T("mixA", [128, 4, 512], BF16)
        wtok = T("wtok", [128, 4, 4], F32)
        Wb = [T(f"Wb{i}", [128, 8, 128], BF16) for i in range(4)]
        work = [T(f"work{i}", [128, 128], F32) for i in range(2)]
        M_all = st.enter_context(nc.sbuf_tensor("M_all", [128, S], BF16))
        MA = [Tl(M_all[:, kc * 512:(kc + 1) * 512], [("M", kc)]) for kc in range(16)]
        Sel = T("Sel", [128, 128], F32)
        negbig = T("negbig", [128, 1], F32)
        mskb = T("mskb", [128, 512], BF16)
        neghalf = T("neghalf", [128, 1], F32)
        Rt = T("Rt", [128, 512], F32)
        Rbc = T("Rbc", [128, 512], F32)
        junkA = T("junkA", [128, 1024], BF16)
        zt = T("zt", [128, 1024], F32)
        mixB = T("mixB", [128, 4, 128], BF16)
        Dg = [T(f"Dg{i}", [128, 4, 128], BF16) for i in range(2)]
        smA = [T(f"smA{i}", [128, 160], F32) for i in range(2)]
        sC = T("sC", [128, 2], F32)
        Irep = T("Irep", [128, 512], BF16)
        smB = T("smB", [128, 8], F32)
        smu = T("smu", [128, 4], U32)
        AR = st.enter_context(nc.sbuf_tensor("AR", [128, 12288], F32))

        def AV(off_kb, kb, dt, pattern=None, **kw):
            a = AR[:, off_kb * 256:(off_kb + kb) * 256]
            if dt != F32:
                a = a.bitcast(dt)
            if pattern:
                a = a.rearrange(pattern, **kw)
            return Tl(a, [("AR", g) for g in range(off_kb, off_kb + kb)])

        xt = [AV(0, 8, BF16, "p (k n) -> p k n", k=8), AV(8, 8, BF16, "p (k n) -> p k n", k=8)]
        posi = AV(16, 2, I32)
        ang = AV(18, 2, F32)
        kint = AV(20, 2, I32)
        r1t = AV(22, 2, F32)
        cost = AV(24, 2, F32)
        sint = AV(26, 2, F32)
        t1t = AV(28, 2, F32)
        t2t = AV(30, 2, F32)
        aqT = AV(32, 4, BF16, "p (c n) -> p c n", c=4)
        akO = AV(36, 2, BF16, "p (c n) -> p c n", c=2)
        akP = AV(38, 2, BF16, "p (c n) -> p c n", c=2)
        agT = T("agT", [128, 4, 512], BF16)
        avO = T("avO", [128, 4, 2, 129], BF16)
        avP = T("avP", [128, 4, 2, 129], BF16)
        Esw = [AV(44, 1, BF16), AV(45, 1, BF16)]
        Emsw = [AV(46, 1, BF16), AV(47, 1, BF16)]
        Esw1 = [T(f"Esw1{h}", [128, 512], BF16) for h in range(2)]
        Emsw1 = [T(f"Emsw1{h}", [128, 512], BF16) for h in range(2)]
        IS = [AV(2 * kc, 2, F32) for kc in range(16)]
        cand = T("cand", [128, 64, 16], F32)
        cand.keys = [("cand", c) for c in range(128)]
        rbuf2 = [[T(f"rbuf{p}{h}", [128, 512], BF16) for h in range(4)] for p in range(2)]
        Ee = [AV(40, 1, BF16), AV(41, 1, BF16)]
        Eo = [AV(42, 1, BF16), AV(43, 1, BF16)]
        Eme = [AV(44, 1, BF16), AV(45, 1, BF16)]
        Emo = [AV(46, 1, BF16), AV(47, 1, BF16)]

        ps = []
        for i in range(8):
            p = st.enter_context(nc.psum_tensor(f"ps{i}", [128, 512], F32))
            ps.append(Tl(p[:], [("ps", i)]))

        pg = Prog(nc)
        cs = cst_t.ap

        def dma(q, out_ap, in_ap, reads, writes):
            pg.op(q, lambda e: e.dma_start(out=out_ap, in_=in_ap), reads, writes, dma=True)

        def mmul(out_ap, lhsT, rhs, start, stop, reads, writes):
            pg.op("pe", lambda e: e.matmul(out_ap, lhsT, rhs, start=start, stop=stop), reads, writes)

        def act(out_ap, in_ap, func, reads, writes, bias=0.0, scale=1.0, accum=None):
            pg.op("act", lambda e: e.activation(out_ap, in_ap, func, bias=bias, scale=scale,
                                                accum_out=accum), reads, writes)

        def ts(eng, out_ap, in_ap, s1, s2, op0, op1, reads, writes, accum=None):
            if accum is None:
                pg.op(eng, lambda e: e.tensor_scalar(out_ap, in_ap, s1, s2, op0=op0, op1=op1), reads, writes)
            else:
                pg.op(eng, lambda e: e.tensor_scalar(out_ap, in_ap, s1, s2, op0=op0, op1=op1,
                                                     accum_out=accum), reads, writes)

        def tt(eng, out_ap, a, b, op, reads, writes):
            pg.op(eng, lambda e: e.tensor_tensor(out=out_ap, in0=a, in1=b, op=op), reads, writes)

        def stt(out_ap, in0, scalar, in1, op0, op1, reads, writes):
            pg.op("dve", lambda e: e.scalar_tensor_tensor(out=out_ap, in0=in0, scalar=scalar, in1=in1,
                                                          op0=op0, op1=op1), reads, writes)

        dma("sp", cst_t.ap, cst, [], [cst_t])
        dma("sp", bias_t.ap, biasT, [], [bias_t])
        dma("pool", wo_t.ap.rearrange("p k n -> p (k n)"), wo, [], [wo_t])
        dma("pool", bo_t.ap[0:1, :], bo, [], [bo_t])
        dma("sp", lng_t.ap, lng.to_broadcast([128, 1024]), [], [lng_t])
        dma("sp", lnb_t.ap, lnb.to_broadcast([128, 1024]), [], [lnb_t])
        dma("pool", wvg_t.ap.rearrange("p k n -> p (k n)"), wvg, [], [wvg_t])
        dma("pool", wvo_t.ap.rearrange("p k n -> p (k n)"), wvo, [], [wvo_t])
        dma("sp", bvg_t.ap, bvg.to_broadcast([128, 64]), [], [bvg_t])
        dma("sp", bvo_t.ap, bvo.to_broadcast([128, 132]), [], [bvo_t])
        dma("sp", snk.ap[64:65, :], sinks[0:1, 0:4], [], [snk])
        dma("sp", snk.ap[32:33, :], sinks[0:1, 4:8], [], [snk])
        pg.op("pool", lambda e: e.memset(ident.ap, 0.0), [], [ident])
        pg.op("pool", lambda e: e.affine_select(out=ident.ap, in_=ident.ap, pattern=[[-1, 128]],
                                                compare_op=ALU.not_equal, fill=1.0, base=0,
                                                channel_multiplier=1), [ident], [ident])
        pg.op("pool", lambda e: e.memset(ones_t.ap, 1.0), [], [ones_t])
        for q in range(4):
            pg.op("pool", (lambda o: lambda e: e.tensor_copy(o, ident.ap))(Irep.ap[:, q * 128:(q + 1) * 128]), [ident], [Irep])
        pg.op("pool", lambda e: e.memset(negbig.ap, -1.0e29), [], [negbig])
        pg.op("dve", lambda e: e.tensor_copy(mskb.ap, cs[:, C_SWM:C_SWM + 512]), [cst_t], [mskb])
        pg.op("pool", lambda e: e.memset(neghalf.ap, -0.5), [], [neghalf])
        for _t in bqTs:
            pg.op("pool", (lambda f: lambda e: e.memset(f, 0.0))(_t.ap.rearrange("p h n -> p (h n)")), [], [_t])
        pg.op("pool", lambda e: e.memset(iqT.ap.rearrange("p h n -> p (h n)"), 0.0), [], [iqT])
        pg.op("pool", lambda e: e.memset(Sel.ap, 0.0), [], [Sel])
        pg.op("pool", lambda e: e.memset(Sel.ap[64:65, 0:64], 1.0), [Sel], [Sel])
        pg.op("pool", lambda e: e.memset(Sel.ap[32:33, 64:128], 1.0), [Sel], [Sel])
        pg.op("pool", lambda e: e.memset(Rt.ap, 0.0), [], [Rt])
        pg.op("pool", lambda e: e.memset(onesf.ap, 1.0), [], [onesf])
        for vt in (Vb, avO, avP):
            flat = vt.ap.rearrange("p a b -> p (a b)") if len(vt.ap.shape) == 3 else vt.ap.rearrange("p a g b -> p (a g b)")
            pg.op("pool", (lambda f: lambda e: e.memset(f, 0.0))(flat), [], [vt])
            v3 = vt.ap if len(vt.ap.shape) == 3 else vt.ap.rearrange("p a g b -> p (a g) b")
            pg.op("pool", (lambda f: lambda e: e.memset(f, 1.0))(v3[:, :, 32:33]), [], [vt])
            pg.op("pool", (lambda f: lambda e: e.memset(f, 1.0))(v3[:, :, 128:129]), [], [vt])
        act(snk.ap[64:65, :], snk.ap[64:65, :], AF.Exp, [snk], [snk])
        act(snk.ap[32:33, :], snk.ap[32:33, :], AF.Exp, [snk], [snk])

        wcount = [0]

        wbf_ready = [False]

        def loadW(cid):
            w = Wb[wcount[0] % 4]
            wcount[0] += 1
            if wbf_ready[0]:
                dma("sp", w.ap.rearrange("p k n -> p (k n)"), wbf[cid], [("wbf", cid)], [w])
            else:
                dma("pool", w.ap.rearrange("p k n -> p (k n)"), wT[cid], [], [w])
            return w

        def cast_weights():
            for cid in range(2, 34):
                dma("pool", wbf[cid], wT[cid], [], [("wbf", cid)])
            wbf_ready[0] = True

        def load_x(buf, src, g):
            dma("pool", buf.ap, src[:, :, g * 512:(g + 1) * 512].rearrange("k p n -> p k n"), [], [buf])

        TSETS = [dict(posi=posi, ang=ang, kint=kint, r1t=r1t, cost=cost, sint=sint),
                 dict(posi=AV(32, 2, I32), ang=AV(34, 2, F32), kint=AV(36, 2, I32), r1t=AV(38, 2, F32),
                      cost=AV(40, 2, F32), sint=AV(42, 2, F32))]

        def tables(pos_src, g, tset=0):
            T_ = TSETS[tset]
            posi, ang, kint, r1t, cost, sint = (T_["posi"], T_["ang"], T_["kint"], T_["r1t"], T_["cost"], T_["sint"])
            dma("sp", posi.ap, pos_src[0:1, g * 512:(g + 1) * 512].to_broadcast([128, 512]), [], [posi])
            ts("dve", ang.ap, posi.ap, cs[:, C_INV:C_INV + 1], None, ALU.mult, ALU.bypass, [posi, cst_t], [ang])
            ts("dve", kint.ap, ang.ap, 1.0 / TWO_PI, None, ALU.mult, ALU.bypass, [ang], [kint])
            stt(r1t.ap, kint.ap, -C1, ang.ap, ALU.mult, ALU.add, [kint, ang], [r1t])
            stt(r1t.ap, kint.ap, -C2, r1t.ap, ALU.mult, ALU.add, [kint, r1t], [r1t])
            ts("dve", r1t.ap, r1t.ap, -PI32, PI32, ALU.max, ALU.min, [r1t], [r1t])
            act(sint.ap, r1t.ap, AF.Sin, [r1t, cst_t], [sint], scale=cs[:, C_SGN:C_SGN + 1])
            ts("dve", kint.ap, ang.ap, 1.0 / TWO_PI, 0.25, ALU.mult, ALU.add, [ang], [kint])
            stt(r1t.ap, kint.ap, -C1, ang.ap, ALU.mult, ALU.add, [kint, ang], [r1t])
            stt(r1t.ap, kint.ap, -C2, r1t.ap, ALU.mult, ALU.add, [kint, r1t], [r1t])
            ts("dve", r1t.ap, r1t.ap, -1.5 * PI32, 0.5 * PI32, ALU.max, ALU.min, [r1t], [r1t])
            act(cost.ap, r1t.ap, AF.Sin, [r1t, cst_t], [cost], bias=cs[:, C_HPI:C_HPI + 1], scale=SHR)

        pbank = [0]

        def projT(xtile, w, bank):
            for k in range(8):
                mmul(ps[bank].ap, w.ap[:, k, :], xtile.ap[:, k, :], k == 0, k == 7, [w, xtile], [ps[bank]])

        t1alt = [t1t, AV(44, 2, F32)]
        t2alt = [t2t, AV(46, 2, F32)]

        PAIRS = [(0, 1), (2, 7)]

        def run_jobs(xtile, jobs):
            for _ in run_jobs_gen(xtile, jobs):
                pass

        def run_jobs_gen(xtile, jobs):
            def preload(job):
                return [loadW(job[1]), loadW(job[2])] if job[0] == "rope" else [loadW(job[1])]
            nxt = preload(jobs[0])
            for n, job in enumerate(jobs):
                cur = nxt
                if n + 1 < len(jobs):
                    nxt = preload(jobs[n + 1])
                b0, b1 = PAIRS[pbank[0] % 2]
                pbank[0] += 1
                if job[0] == "rope":
                    _, cid, cidp, out_tl, out_ap = job
                    ta, tb = t1alt[n % 2], t2alt[n % 2]
                    projT(xtile, cur[0], b0)
                    projT(xtile, cur[1], b1)
                    stt(ta.ap, ps[b0].ap, bias_t.ap[:, cid:cid + 1], cost.ap, ALU.add, ALU.mult,
                        [ps[b0], bias_t, cost], [ta])
                    stt(tb.ap, ps[b1].ap, bias_t.ap[:, cidp:cidp + 1], sint.ap, ALU.add, ALU.mult,
                        [ps[b1], bias_t, sint], [tb])
                    if isinstance(out_ap, tuple):
                        tt("pool", out_ap[0], ta.ap[0:64, :], tb.ap[0:64, :], ALU.add, [ta, tb], [out_tl])
                        tt("pool", out_ap[1], ta.ap[64:128, :], tb.ap[64:128, :], ALU.add, [ta, tb], [out_tl])
                    else:
                        tt("pool", out_ap, ta.ap, tb.ap, ALU.add, [ta, tb], [out_tl])
                else:
                    _, cid, out_tl, out_ap = job
                    projT(xtile, cur[0], b0)
                    act(out_ap, ps[b0].ap, AF.Silu, [ps[b0], bias_t], [out_tl], bias=bias_t.ap[:, cid:cid + 1])
                yield

        vbank = [0]

        def vproj(xtile, tt_, wtile, ncol):
            bank = (1, 7)[vbank[0] % 2]
            vbank[0] += 1
            for k in range(8):
                mmul(ps[bank].ap[:, 0:ncol], xtile.ap[:, k, tt_ * 128:(tt_ + 1) * 128], wtile.ap[:, k, :],
                     k == 0, k == 7, [xtile, wtile], [ps[bank]])
            return bank

        WG = [None] * 2
        load_x(xt[0], xg, 0)
        for c in range(2):
            WG[c] = loadW(c)
        for g in range(16):
            xb = xt[g % 2]
            if g + 1 < 16:
                load_x(xt[(g + 1) % 2], xg, g + 1)
            tables(posg, g, g % 2)
            gcost, gsint = TSETS[g % 2]["cost"], TSETS[g % 2]["sint"]
            b0 = pbank[0] % 2 * 2
            pbank[0] += 1
            projT(xb, WG[0], b0)
            projT(xb, WG[1], b0 + 1)
            ta, tb = t1alt[g % 2], t2alt[g % 2]
            stt(ta.ap, ps[b0].ap, bias_t.ap[:, 0:1], gcost.ap, ALU.add, ALU.mult, [ps[b0], bias_t, gcost], [ta])
            stt(tb.ap, ps[b0 + 1].ap, bias_t.ap[:, 1:2], gsint.ap, ALU.add, ALU.mult, [ps[b0 + 1], bias_t, gsint], [tb])
            tt("pool", KT.ap[:, g * 512:(g + 1) * 512], ta.ap, tb.ap, ALU.add, [ta, tb], [KT])
            for t4 in range(4):
                bank = vproj(xb, t4, wvg_t, 64)
                tt("dve", Vb.ap[:, g * 4 + t4, 64:128], ps[bank].ap[:, 0:64], bvg_t.ap, ALU.add,
                   [ps[bank], bvg_t], [Vb])

        def normalize(bank_e, bank_o, gate_tl, gate_ap, out_tl, out_ap, with_sink, bb0):
            pe_, po_ = ps[bank_e], ps[bank_o]
            if with_sink:
                tt("dve", Rt.ap[64:65, :].rearrange("p (h t) -> p h t", h=4),
                   pe_.ap[64:65, :].rearrange("p (h t) -> p h t", h=4),
                   snk.ap[64:65, :].unsqueeze(2).to_broadcast([1, 4, 128]), ALU.add, [pe_, snk], [Rt])
                tt("dve", Rt.ap[32:33, :].rearrange("p (h t) -> p h t", h=4),
                   po_.ap[32:33, :].rearrange("p (h t) -> p h t", h=4),
                   snk.ap[32:33, :].unsqueeze(2).to_broadcast([1, 4, 128]), ALU.add, [po_, snk], [Rt])
            else:
                act(Rt.ap[64:65, :], pe_.ap[64:65, :], AF.Identity, [pe_], [Rt])
                act(Rt.ap[32:33, :], po_.ap[32:33, :], AF.Identity, [po_], [Rt])
            mmul(ps[bb0].ap, Sel.ap, Rt.ap, True, True, [Sel, Rt], [ps[bb0]])
            pg.op("dve", lambda e: e.reciprocal(Rbc.ap, ps[bb0].ap), [ps[bb0]], [Rbc])
            tt("dve", Rbc.ap[0:64, :], pe_.ap[0:64, :], Rbc.ap[0:64, :], ALU.mult, [pe_, Rbc], [Rbc])
            tt("dve", Rbc.ap[64:128, :], po_.ap[64:128, :], Rbc.ap[64:128, :], ALU.mult, [po_, Rbc], [Rbc])
            tt("pool", out_ap, Rbc.ap.rearrange("p (h t) -> p h t", h=4), gate_ap, ALU.mult,
               [Rbc, gate_tl], [out_tl])

        def stageA(i, bl):
            NC_ = i + 1
            CS = 8 * NC_
            par = i % 2
            sA, dg = smA[par], Dg[par]
            tcols = slice(bl * 128, (bl + 1) * 128)
            wv = wtok.ap[:, bl, :]
            absw, sgn = sA.ap[:, 0:4], sA.ap[:, 4:8]
            ts("dve", absw, wv, -1.0, None, ALU.mult, ALU.bypass, [wtok], [sA])
            tt("dve", absw, absw, wv, ALU.max, [wtok, sA], [sA])
            ts("dve", sgn, wv, 0.0, 2.0, ALU.is_ge, ALU.mult, [wtok], [sA])
            ts("dve", sgn, sgn, -1.0, None, ALU.add, ALU.bypass, [sA], [sA])
            for h in range(4):
                ts("dve", dg.ap[:, h, :], ident.ap, sA.ap[:, 4 + h:5 + h], None, ALU.mult, ALU.bypass,
                   [ident, sA], [dg])
            yield
            for kc in range(NC_):
                pb = [(0, 1, 2, 7)[(h + kc) % 4] for h in range(4)]
                rbuf = rbuf2[kc % 2]
                for h in range(4):
                    mmul(ps[pb[h]].ap, iqT.ap[:, h, tcols], KT.ap[:, kc * 512:(kc + 1) * 512],
                         True, True, [iqT, KT], [ps[pb[h]]])
                yield
                for h in range(4):
                    if h % 2 == 0:
                        act(rbuf[h].ap, ps[pb[h]].ap, AF.Relu, [ps[pb[h]], sA], [rbuf[h]], scale=sA.ap[:, h:h + 1])
                    else:
                        ts("dve", rbuf[h].ap, ps[pb[h]].ap, sA.ap[:, h:h + 1], 0.0, ALU.mult, ALU.max,
                           [ps[pb[h]], sA], [rbuf[h]])
                yield
                ab = pb[0]
                for h in range(4):
                    mmul(ps[ab].ap, dg.ap[:, h, :], rbuf[h].ap, h == 0, h == 3, [dg, rbuf[h]], [ps[ab]])
                stt(IS[kc].ap, ps[ab].ap, float(-1e-30 * 512 * kc), cs[:, C_EPS:C_EPS + 512], ALU.add, ALU.add,
                    [ps[ab], cst_t], [IS[kc]])
                if kc == NC_ - 1:
                    tt("dve", IS[kc].ap, IS[kc].ap, cs[:, C_CNEG:C_CNEG + 512], ALU.min, [IS[kc], cst_t], [IS[kc]])
                yield
            isflat = AR[:, 0:512 * NC_]
            iskeys = [IS[k] for k in range(NC_)]
            cflat = cand.ap.rearrange("p c e -> p (c e)")
            nlo, hi, wdt, nmid, cnt, lo = (sA.ap[:, 8:9], sA.ap[:, 9:10], sA.ap[:, 10:11], sA.ap[:, 11:12],
                                            sA.ap[:, 12:13], sA.ap[:, 13:14])
            if NC_ <= 2:
                nbis = NBIS + 4
                cflat = isflat
                ckeys = iskeys
                jk = junkA.ap[:, 0:512 * NC_]
                pg.op("dve", lambda e: e.tensor_reduce(out=hi, in_=isflat, op=ALU.max, axis=mybir.AxisListType.X), iskeys, [sA])
                ts("dve", cand.ap.rearrange("p c e -> p (c e)")[:, 0:512 * NC_], isflat, -1.0e29, 3.0e30, ALU.is_lt, ALU.mult,
                   iskeys, [cand])
                tt("dve", cand.ap.rearrange("p c e -> p (c e)")[:, 0:512 * NC_], cand.ap.rearrange("p c e -> p (c e)")[:, 0:512 * NC_],
                   isflat, ALU.add, iskeys + [cand], [cand])
                pg.op("dve", lambda e: e.tensor_reduce(out=lo, in_=cand.ap.rearrange("p c e -> p (c e)")[:, 0:512 * NC_],
                                                       op=ALU.min, axis=mybir.AxisListType.X), [cand], [sA])
            else:
                nbis = NBIS
                ckeys = [cand]
                jk = junkA.ap
                cand8 = cflat.rearrange("p (c e) -> p c e", e=8)
                CS2 = CS // 2
                for c in range(128):
                    src = isflat[:, c * CS2:(c + 1) * CS2]
                    pg.op("dve", (lambda o, s_: lambda e: e.max(out=o, in_=s_))(cand8[:, c, :], src), iskeys, [("cand", c)])
                    if c % 4 == 3:
                        yield
                chi = cand8[:, :, 1:2].rearrange("p c e -> p (c e)")
                clo = cand8[:, :, 3:4].rearrange("p c e -> p (c e)")
                tv = sA.ap[:, 32:160]
                pg.op("dve", lambda e: e.tensor_reduce(out=hi, in_=chi, op=ALU.max, axis=mybir.AxisListType.X), [cand], [sA])
                ts("dve", tv, clo, -1.0e29, 3.0e30, ALU.is_lt, ALU.mult, [cand], [sA])
                tt("dve", tv, tv, clo, ALU.add, [cand, sA], [sA])
                pg.op("dve", lambda e: e.tensor_reduce(out=lo, in_=tv, op=ALU.min, axis=mybir.AxisListType.X), [sA], [sA])
            tt("dve", wdt, hi, lo, ALU.subtract, [sA], [sA])
            yield
            mid = nmid
            for it in range(nbis):
                stt(mid, wdt, float(0.5 ** (it + 1)), lo, ALU.mult, ALU.add, [sA], [sA])
                ts("dve", jk, cflat, mid, None, ALU.is_ge, ALU.add, ckeys + [sA, junkA], [junkA, sA], accum=cnt)
                ts("dve", smu.ap[:, 0:1], cnt, 255.5, None, ALU.is_gt, ALU.bypass, [sA], [smu])
                pg.op("dve", lambda e: e.copy_predicated(lo, smu.ap[:, 0:1], mid), [sA, smu], [sA])
                yield
            for kc in range(NC_):
                ts("dve", MA[kc].ap, IS[kc].ap, lo, -32768.0, ALU.is_lt, ALU.mult, [IS[kc], sA], [MA[kc]])
            yield

        def unitsA(i):
            return (1 + 3 * (i + 1) + 1 + NBIS + 4 + 1) if i <= 1 else (1 + 3 * (i + 1) + 32 + 1 + NBIS + 1)

        def stageB(i, bl):
            NC_ = i + 1
            NK = 4 * NC_
            tcols = slice(bl * 128, (bl + 1) * 128)
            dma("sp", zt.ap, xtok[i * 128:(i + 1) * 128, :], [], [zt])
            bqT = bqTs[(i // 4) % 2]
            bgT = bgTs[(i // 4) % 2]
            bq2 = bqT.ap.rearrange("p (m two) n -> p two m n", two=2)
            Ebuf = [Ee[0], Ee[1], Eo[0], Eo[1]]

            for half in range(2):
                def qk(kt):
                    ks = slice(kt * 128, (kt + 1) * 128)
                    bank = 3 + kt % 2
                    mmul(ps[bank].ap, KT.ap[:, ks], bq2[:, half][:, :, tcols], True, False, [KT, bqT], [ps[bank]])
                    mmul(ps[bank].ap, MA[kt // 4].ap[:, (kt % 4) * 128:(kt % 4 + 1) * 128], Irep.ap, False, True,
                         [MA[kt // 4], Irep], [ps[bank]])

                def pv(kt):
                    em = Ebuf[kt % 4]
                    if half == 0:
                        mmul(ps[5].ap[0:65, :], Vb.ap[:, kt, 64:129], em.ap, kt == 0, kt == NK - 1, [Vb, em], [ps[5]])
                    else:
                        mmul(ps[6].ap, Vb.ap[:, kt, 0:128], em.ap, kt == 0, kt == NK - 1, [Vb, em], [ps[6]])

                qk(0)
                for kt in range(NK):
                    if kt + 1 < NK:
                        qk(kt + 1)
                    if kt >= 1:
                        pv(kt - 1)
                    eb = Ebuf[kt % 4]
                    act(eb.ap, ps[3 + kt % 2].ap, AF.Exp, [ps[3 + kt % 2]], [eb], scale=0.125)
                    yield
                pv(NK - 1)
                yield
            normalize(5, 6, bgT, bgT.ap[:, :, tcols], mixB, mixB.ap, False, 3)
            yield
            for hf in range(2):
                bank = 3 + hf
                cols = slice(hf * 512, (hf + 1) * 512)
                for c in range(8):
                    lhsT = mixA.ap[:, c, tcols] if c < 4 else mixB.ap[:, c - 4, :]
                    mmul(ps[bank].ap, lhsT, wo_t.ap[:, c, cols], c == 0, False, [mixA, mixB, wo_t], [ps[bank]])
                mmul(ps[bank].ap, ones_t.ap[0:1, :], bo_t.ap[0:1, cols], False, True, [ones_t, bo_t], [ps[bank]])
                stt(zt.ap[:, cols], zt.ap[:, cols], ALPHA, ps[bank].ap, ALU.mult, ALU.add, [zt, ps[bank]], [zt])
            yield
            ssum, ssq, mean, var = smB.ap[:, 0:1], smB.ap[:, 1:2], smB.ap[:, 2:3], smB.ap[:, 3:4]
            pg.op("pool", lambda e: e.memset(smB.ap[:, 0:2], 0.0), [], [smB])
            act(junkA.ap, zt.ap, AF.Identity, [zt, junkA], [junkA, smB], accum=ssum)
            act(junkA.ap, zt.ap, AF.Square, [zt, junkA], [junkA, smB], accum=ssq)
            ts("dve", mean, ssum, 1.0 / 1024, None, ALU.mult, ALU.bypass, [smB], [smB])
            stt(var, mean, -1.0, mean, ALU.mult, ALU.mult, [smB], [smB])
            stt(var, ssq, 1.0 / 1024, var, ALU.mult, ALU.add, [smB], [smB])
            ts("dve", var, var, LN_EPS, None, ALU.add, ALU.bypass, [smB], [smB])
            pg.op("pool", lambda e: e.tensor_tensor(out=var, in0=var, in1=neghalf.ap, op=ALU.pow), [smB, neghalf], [smB])
            ts("dve", zt.ap, zt.ap, mean, var, ALU.subtract, ALU.mult, [zt, smB], [zt])
            tt("pool", zt.ap, zt.ap, lng_t.ap, ALU.mult, [zt, lng_t], [zt])
            tt("pool", zt.ap, zt.ap, lnb_t.ap, ALU.add, [zt, lnb_t], [zt])
            dma("sp", out[i * 128:(i + 1) * 128, :], zt.ap, [zt], [])
            yield

        def unitsB(i):
            return 8 * (i + 1) + 6

        def drain(g):
            for _ in g:
                pass

        def merge(ga, na, gb, nb):
            da = db = 0
            alive_a, alive_b = ga is not None, True
            while alive_a or alive_b:
                take_a = alive_a and (not alive_b or da * nb <= db * na)
                if take_a:
                    try:
                        next(ga)
                        da += 1
                    except StopIteration:
                        alive_a = False
                else:
                    try:
                        next(gb)
                        db += 1
                    except StopIteration:
                        alive_b = False

        def opass_gen(og):
            bqT, bgT = bqTs[og % 2], bgTs[og % 2]
            xb = xt[0]
            load_x(xb, xo, og)
            load_x(xt[1], xp, og)
            tables(poso, og)
            yield
            jobs = []
            jobs += [("rope", 10 + m, 12 + m, iqT, (iqT.ap[64:128, 2 * m, :], iqT.ap[64:128, 2 * m + 1, :])) for m in range(2)]
            jobs += [("rope", 14 + m, 18 + m, aqT, aqT.ap[:, m, :]) for m in range(4)]
            jobs += [("rope", 22 + m, 24 + m, akO, akO.ap[:, m, :]) for m in range(2)]
            jobs += [("rope", 2 + m, 6 + m, bqT, (bqT.ap[0:64, 2 * m, :], bqT.ap[0:64, 2 * m + 1, :])) for m in range(4)]
            jobs += [("gate", 26 + m, agT, agT.ap[:, m, :]) for m in range(4)]
            jobs += [("gate", 30 + m, bgT, bgT.ap[:, m, :]) for m in range(4)]
            for _ in run_jobs_gen(xb, jobs):
                yield
            for t4 in range(4):
                bank = vproj(xb, t4, wvo_t, 132)
                tt("dve", avO.ap[:, t4, :, 64:128], ps[bank].ap[:, 0:128].rearrange("p (g d) -> p g d", g=2),
                   bvo_t.ap[:, 0:128].rearrange("p (g d) -> p g d", g=2), ALU.add, [ps[bank], bvo_t], [avO])
                tt("dve", wtok.ap[:, t4, :], ps[bank].ap[:, 128:132], bvo_t.ap[:, 128:132], ALU.add,
                   [ps[bank], bvo_t], [wtok])
                ts("dve", wtok.ap[:, t4, :], wtok.ap[:, t4, :], 1.0 / 16.0, None, ALU.mult, ALU.bypass, [wtok], [wtok])
                yield
            xb = xt[1]
            tables(posp, og)
            yield
            for _ in run_jobs_gen(xb, [("rope", 22 + m, 24 + m, akP, akP.ap[:, m, :]) for m in range(2)]):
                yield
            for t4 in range(4):
                bank = vproj(xb, t4, Tl(wvo_t.ap[:, :, 0:128], wvo_t.keys), 128)
                tt("dve", avP.ap[:, t4, :, 64:128], ps[bank].ap[:, 0:128].rearrange("p (g d) -> p g d", g=2),
                   bvo_t.ap[:, 0:128].rearrange("p (g d) -> p g d", g=2), ALU.add, [ps[bank], bvo_t], [avP])
                yield

        NOPASS = 1 + 20 + 4 + 1 + 2 + 4

        for og in range(4):
            def swa_gen(og=og, piped=(og != 0)):
                def banks(bl, g):
                    if not piped:
                        return (3, 4), (5, 6), 3, (Esw, Emsw)
                    sb = (3, 4) if g == 0 else (2, 7)
                    ob = (5, 6) if bl % 2 == 0 else (0, 1)
                    return sb, ob, (7 if bl % 2 == 0 else 2), ((Esw, Emsw) if g == 0 else (Esw1, Emsw1))

                def s1(bl, g):
                    sb, _, _, _ = banks(bl, g)
                    tcols = slice(bl * 128, (bl + 1) * 128)
                    for half in range(2):
                        rows = slice(64 * half, 64 * half + 64)
                        for kt2, aksrc in enumerate((akP, akO)):
                            mmul(ps[sb[half]].ap[:, kt2 * 256:(kt2 + 1) * 256],
                                 aksrc.ap[rows, g, tcols], aqT.ap[rows, 2 * g:2 * g + 2, tcols], True, True,
                                 [aksrc, aqT], [ps[sb[half]]])

                def s2(bl, g):
                    sb, _, _, (E_, Em_) = banks(bl, g)
                    mcol = 256 if (og == 0 and bl == 0) else 0
                    for half in range(2):
                        act(E_[half].ap, ps[sb[half]].ap, AF.Exp, [ps[sb[half]]], [E_[half]], scale=0.125)
                        e4 = E_[half].ap.rearrange("p (k h t) -> p k h t", k=2, h=2)
                        o4 = Em_[half].ap.rearrange("p (k h t) -> p k h t", k=2, h=2)
                        for kt2 in range(2):
                            mk = bc_mid(mskb.ap[:, mcol + kt2 * 128:mcol + kt2 * 128 + 128], 2)
                            tt("dve", o4[:, kt2], e4[:, kt2], mk, ALU.mult, [E_[half], mskb], [Em_[half]])

                def s3(bl, g):
                    _, ob, _, (E_, Em_) = banks(bl, g)
                    for kt2, avsrc in enumerate((avP, avO)):
                        mmul(ps[ob[0]].ap[0:65, g * 256:(g + 1) * 256], avsrc.ap[:, bl, g, 64:129],
                             Em_[0].ap[:, kt2 * 256:(kt2 + 1) * 256],
                             kt2 == 0, kt2 == 1, [avsrc, Em_[0]], [ps[ob[0]]])
                        mmul(ps[ob[1]].ap[:, g * 256:(g + 1) * 256], avsrc.ap[:, bl, g, 0:128],
                             Em_[1].ap[:, kt2 * 256:(kt2 + 1) * 256],
                             kt2 == 0, kt2 == 1, [avsrc, Em_[1]], [ps[ob[1]]])

                def s4(bl):
                    _, ob, bb0, _ = banks(bl, 0)
                    tcols = slice(bl * 128, (bl + 1) * 128)
                    normalize(ob[0], ob[1], agT, agT.ap[:, :, tcols], mixA, mixA.ap[:, :, tcols], True, bb0)

                if not piped:
                    for bl in range(4):
                        for g in range(2):
                            s1(bl, g)
                            s2(bl, g)
                            s3(bl, g)
                            yield
                        s4(bl)
                        yield
                else:
                    s1(0, 0)
                    s1(0, 1)
                    s2(0, 0)
                    s2(0, 1)
                    for bl in range(4):
                        if bl + 1 < 4:
                            s1(bl + 1, 0)
                            s1(bl + 1, 1)
                        s3(bl, 0)
                        s3(bl, 1)
                        if bl + 1 < 4:
                            s2(bl + 1, 0)
                            s2(bl + 1, 1)
                        s4(bl)
                        yield

            if og == 0:
                drain(opass_gen(0))
                cast_weights()
                drain(stageA(0, 0))
                drain(swa_gen(piped=True))
            else:
                drain(swa_gen())
            for bl in range(4):
                i = og * 4 + bl
                if bl < 3:
                    merge(stageA(i + 1, bl + 1), unitsA(i + 1), stageB(i, bl), unitsB(i))
                elif og < 3:
                    drain(opass_gen(og + 1))
                    merge(stageA(i + 1, 0), unitsA(i + 1), stageB(i, bl), unitsB(i))
                else:
                    drain(stageB(i, bl))

        counts = pg.emit(st)
    return nc, counts


_CACHE = {}


def _chunks():
    AQ, AK, AV_, AG, BQ, BK, BV, BG, IQ, IK, IW = 0, 512, 640, 768, 1280, 1792, 1856, 1920, 2432, 2688, 2752
    perm = (np.arange(64) + 32) % 64
    r64 = np.arange(64)

    def heads(base, hl, p):
        return np.concatenate([base + 64 * h + (perm if p else r64) for h in hl])

    ch = []
    ch += [np.concatenate([BK + r64, IK + r64]), np.concatenate([BK + perm, IK + perm])]
    ch += [heads(BQ, [2 * m, 2 * m + 1], False) for m in range(4)]
    ch += [heads(BQ, [2 * m, 2 * m + 1], True) for m in range(4)]
    ch += [heads(IQ, [2 * m, 2 * m + 1], False) for m in range(2)]
    ch += [heads(IQ, [2 * m, 2 * m + 1], True) for m in range(2)]
    ch += [heads(AQ, [2 * m, 2 * m + 1], False) for m in range(4)]
    ch += [heads(AQ, [2 * m, 2 * m + 1], True) for m in range(4)]
    ch += [heads(AK, [g, g], False) for g in range(2)]
    ch += [heads(AK, [g, g], True) for g in range(2)]
    ch += [AG + 128 * m + np.arange(128) for m in range(4)]
    ch += [BG + 128 * m + np.arange(128) for m in range(4)]
    return ch, (AV_, BV, IW)


def kernel(x, positions, w_in, b_in, swa_sinks, w_out, b_out, ln_gain, ln_bias):
    x = np.asarray(x, dtype=np.float32)
    positions = np.asarray(positions, dtype=np.int32)
    w = np.asarray(w_in, dtype=np.float32)[0]
    b = np.asarray(b_in, dtype=np.float32)[0]
    sk = np.asarray(swa_sinks, dtype=np.float32)[0]
    wo_ = np.asarray(w_out, dtype=np.float32)[0]
    bo_ = np.asarray(b_out, dtype=np.float32)
    lg = np.asarray(ln_gain, dtype=np.float32)
    lb = np.asarray(ln_bias, dtype=np.float32)

    if "nc" not in _CACHE:
        _CACHE["nc"] = build()[0]
    nc = _CACHE["nc"]

    ch, (AV_, BV, IW) = _chunks()
    wT = np.stack([np.ascontiguousarray(w[:, c].reshape(8, 128, 128).transpose(1, 0, 2)).reshape(128, 1024) for c in ch])
    biasT = np.ascontiguousarray(np.stack([b[c] for c in ch], axis=1))
    cg = BV + np.arange(64)
    co = np.concatenate([AV_ + np.arange(128), IW + np.arange(4)])
    wvg = np.ascontiguousarray(w[:, cg].reshape(8, 128, 64).transpose(1, 0, 2)).reshape(128, 512)
    wvo = np.ascontiguousarray(w[:, co].reshape(8, 128, 132).transpose(1, 0, 2)).reshape(128, 8 * 132)
    bvg = np.ascontiguousarray(b[cg][None, :])
    bvo = np.ascontiguousarray(b[co][None, :])
    wo_h = np.ascontiguousarray(wo_.reshape(8, 128, 1024).transpose(1, 0, 2)).reshape(128, 8192)
    sinks = np.ascontiguousarray(np.concatenate([sk[0::2], sk[1::2]])[None, :])

    inv = (np.float32(10000.0) ** (-np.arange(0, 64, 2, dtype=np.float32) / np.float32(64))).astype(np.float32)
    pidx = np.arange(128) % 64
    cst_base = np.zeros((128, C_END), np.float32)
    cst_base[:, C_INV] = inv[pidx % 32]
    cst_base[:, C_SGN] = np.where(pidx < 32, -1.0, 1.0) * SHR
    cst_base[:, C_HPI] = (math.pi / 2) * SHR
    cst_base[:, C_EPS:C_EPS + 512] = (-1e-30 * np.arange(512, dtype=np.float64)).astype(np.float32)[None, :]
    s_i = np.arange(128)[:, None]
    t_i = np.arange(128)[None, :]
    swm = np.concatenate([(s_i > t_i).astype(np.float32), (s_i <= t_i).astype(np.float32)], axis=1)

    in_maps = []
    own_idx = []
    for c in range(8):
        bb, j = c // 4, c % 4
        blocks = 4 * np.arange(16) + j
        oi = (blocks[:, None] * 128 + np.arange(128)[None, :]).reshape(-1)
        pb = np.maximum(blocks - 1, 0)
        pi = (pb[:, None] * 128 + np.arange(128)[None, :]).reshape(-1)
        own_idx.append(oi)
        xT = np.ascontiguousarray(x[bb].T).reshape(8, 128, S)
        cst = cst_base.copy()
        cn = np.full((128, 512), -1e30, np.float32)
        for o in range(4):
            if o < j:
                cn[:, o * 128:(o + 1) * 128] = 1e30
            elif o == j:
                cn[:, o * 128:(o + 1) * 128] = np.where(s_i.T <= t_i.T, 1e30, -1e30)
        cst[:, C_CNEG:C_CNEG + 512] = cn
        cst[:, C_SWM:C_SWM + 256] = swm
        swf = swm.copy()
        if j == 0:
            swf[:, 0:128] = 0.0
        cst[:, C_SWMF:C_SWMF + 256] = swf
        in_maps.append({
            "xg": xT,
            "xo": np.ascontiguousarray(xT[:, :, oi]),
            "xp": np.ascontiguousarray(xT[:, :, pi]),
            "xtok": np.ascontiguousarray(x[bb][oi]),
            "posg": np.ascontiguousarray(positions[bb][None, :]),
            "poso": np.ascontiguousarray(positions[bb][oi][None, :]),
            "posp": np.ascontiguousarray(positions[bb][pi][None, :]),
            "wT": wT, "biasT": biasT, "wvg": wvg, "wvo": wvo, "bvg": bvg, "bvo": bvo,
            "wo": wo_h, "bo": bo_, "lng": lg, "lnb": lb, "sinks": sinks, "cst": cst,
        })
    res = run_bass_kernel_spmd(nc, in_maps, core_ids=list(range(8)))
    outp = np.empty((2, S, 1024), np.float32)
    for c in range(8):
        outp[c // 4, own_idx[c]] = res.results[c]["out"]
    return outp
```

```python
import math
from contextlib import ExitStack
import numpy as np
import concourse.bass as bass
import concourse.mybir as mybir
from concourse.bass_utils import run_bass_kernel_spmd

F32 = mybir.dt.float32
BF16 = mybir.dt.bfloat16
I32 = mybir.dt.int32
U32 = mybir.dt.uint32
ALU = mybir.AluOpType
AF = mybir.ActivationFunctionType
ENGS = ("pe", "act", "dve", "pool", "sp")

S = 8192
NOWN = 2048
ALPHA = 2.0 ** 0.25
LN_EPS = 1e-5
NBIS = 7
TWO_PI = 2.0 * math.pi
C1 = 6.28125
C2 = TWO_PI - 6.28125
SHR = 1.0 - 2e-6
PI32 = float(np.float32(math.pi))
C_INV, C_SGN, C_HPI, C_EPS, C_CNEG, C_SWM, C_SWMF, C_END = 0, 1, 2, 3, 515, 1027, 1283, 1539


class Tl:
    def __init__(self, ap, keys):
        self.ap = ap
        self.keys = list(keys)

    def __getitem__(self, idx):
        return self.ap[idx]


class Prog:
    def __init__(self, nc, n_dma_sems=8):
        self.nc = nc
        self.eng = {"pe": nc.tensor, "act": nc.scalar, "dve": nc.vector,
                    "pool": nc.gpsimd, "sp": nc.sync}
        self.ops = []
        self.last_w = {}
        self.readers = {}
        self.n_dma_sems = n_dma_sems

    @staticmethod
    def _keys(lst):
        out = []
        for k in lst:
            if isinstance(k, Tl):
                out.extend(k.keys)
            elif isinstance(k, (list, tuple)) and len(k) and isinstance(k[0], (list, tuple, Tl)):
                out.extend(Prog._keys(k))
            else:
                out.append(k)
        return out

    def op(self, eng, fn, reads=(), writes=(), dma=False):
        reads = self._keys(reads)
        writes = self._keys(writes)
        deps = set()
        for k in reads:
            if k in self.last_w:
                deps.add(self.last_w[k])
        for k in writes:
            if k in self.last_w:
                deps.add(self.last_w[k])
            for r in self.readers.get(k, ()):
                deps.add(r)
        oid = len(self.ops)
        self.ops.append([eng, fn, sorted(deps), dma])
        for k in reads:
            self.readers.setdefault(k, []).append(oid)
        for k in writes:
            self.last_w[k] = oid
            self.readers[k] = []
        return oid

    def emit(self, stack):
        nc = self.nc
        ops = self.ops
        n = len(ops)
        needed = [False] * n
        for rec in ops:
            eng, fn, deps, dma = rec
            latest = {}
            keep = []
            for d in deps:
                deng, _, _, ddma = ops[d]
                if ddma:
                    keep.append(d)
                    continue
                if deng == "pe" and eng == "pe":
                    continue
                if deng not in latest or d > latest[deng]:
                    latest[deng] = d
            keep.extend(latest.values())
            rec[2] = sorted(keep)
            for d in keep:
                needed[d] = True
        sems = {e: stack.enter_context(nc.semaphore("s_" + e)) for e in ENGS}
        dma_sems = {e: [stack.enter_context(nc.semaphore(f"d_{e}_{i}"))
                        for i in range(self.n_dma_sems)] for e in ("sp", "pool", "act")}
        sig_count = {e: 0 for e in ENGS}
        dma_count = {e: 0 for e in ENGS}
        handle = [None] * n
        waited = {e: {} for e in ENGS}
        last_dma = {}

        def wait(e, sem, val):
            key = id(sem)
            if waited[e].get(key, 0) >= val:
                return
            self.eng[e].wait_ge(sem, val)
            waited[e][key] = val

        for i, (eng, fn, deps, dma) in enumerate(ops):
            for d in deps:
                deng, _, _, ddma = ops[d]
                if deng == "pe" and eng == "pe" and not ddma:
                    continue
                h = handle[d]
                wait(eng, h[0], h[1])
            if dma:
                k = dma_count[eng]
                dma_count[eng] += 1
                sem = dma_sems[eng][k % self.n_dma_sems]
                rnd = k // self.n_dma_sems
                if rnd > 0:
                    wait(eng, sem, 16 * rnd)
                ins = fn(self.eng[eng])
                ins.then_inc(sem, 16)
                handle[i] = (sem, 16 * (rnd + 1))
                last_dma[(eng, k % self.n_dma_sems)] = handle[i]
            else:
                ins = fn(self.eng[eng])
                if needed[i]:
                    sig_count[eng] += 1
                    ins.then_inc(sems[eng], 1)
                    handle[i] = (sems[eng], sig_count[eng])
        for (eng, _), h in last_dma.items():
            wait(eng, h[0], h[1])
        return sig_count, dma_count


def bc_mid(ap2d, n):
    return ap2d.unsqueeze(1).to_broadcast([ap2d.shape[0], n, ap2d.shape[1]])


def build():
    nc = bass.Bass("TRN2", target_bir_lowering=False, dynamic_dma_scratch_size=4096)

    def D(name, shape, dt, kind="ExternalInput"):
        return nc.dram_tensor(name, shape, dt, kind=kind).ap()

    xg = D("xg", [8, 128, S], F32)
    xo = D("xo", [8, 128, NOWN], F32)
    xp = D("xp", [8, 128, NOWN], F32)
    xtok = D("xtok", [NOWN, 1024], F32)
    posg = D("posg", [1, S], I32)
    poso = D("poso", [1, NOWN], I32)
    posp = D("posp", [1, NOWN], I32)
    wT = D("wT", [34, 128, 1024], F32)
    biasT = D("biasT", [128, 34], F32)
    wvg = D("wvg", [128, 8 * 64], F32)
    wvo = D("wvo", [128, 8 * 132], F32)
    bvg = D("bvg", [1, 64], F32)
    bvo = D("bvo", [1, 132], F32)
    wo = D("wo", [128, 8 * 1024], F32)
    bo = D("bo", [1, 1024], F32)
    lng = D("lng", [1, 1024], F32)
    lnb = D("lnb", [1, 1024], F32)
    sinks = D("sinks", [1, 8], F32)
    cst = D("cst", [128, C_END], F32)
    out = D("out", [NOWN, 1024], F32, kind="ExternalOutput")
    wbf = D("wbf", [34, 128, 1024], BF16, kind="Internal")

    with ExitStack() as st:
        def T(name, shape, dt):
            t = st.enter_context(nc.sbuf_tensor(name, shape, dt))
            return Tl(t[:], [name])

        KT = T("KT", [128, S], BF16)
        Vb = T("Vb", [128, 64, 129], BF16)
        wo_t = T("wo_t", [128, 8, 1024], BF16)
        bo_t = T("bo_t", [128, 1024], BF16)
        lng_t = T("lng_t", [128, 1024], F32)
        lnb_t = T("lnb_t", [128, 1024], F32)
        cst_t = T("cst_t", [128, C_END], F32)
        bias_t = T("bias_t", [128, 34], F32)
        ident = T("ident", [128, 128], BF16)
        ones_t = T("ones_t", [128, 128], BF16)
        onesf = T("onesf", [128, 64], F32)
        wvg_t = T("wvg_t", [128, 8, 64], BF16)
        wvo_t = T("wvo_t", [128, 8, 132], BF16)
        bvg_t = T("bvg_t", [128, 64], F32)
        bvo_t = T("bvo_t", [128, 132], F32)
        snk = T("snk", [128, 4], F32)
        bqTs = [T(f"bqT{p}", [128, 8, 512], BF16) for p in range(2)]
        iqT = T("iqT", [128, 4, 512], BF16)
        bgTs = [T(f"bgT{p}", [128, 4, 512], BF16) for p in range(2)]
        mixA = T("mixA", [128, 4, 512], BF16)
        wtok = T("wtok", [128, 4, 4], F32)
        Wb = [T(f"Wb{i}", [128, 8, 128], BF16) for i in range(4)]
        work = [T(f"work{i}", [128, 128], F32) for i in range(2)]
        M_all = st.enter_context(nc.sbuf_tensor("M_all", [128, S], BF16))
        MA = [Tl(M_all[:, kc * 512:(kc + 1) * 512], [("M", kc)]) for kc in range(16)]
        Sel = T("Sel", [128, 128], F32)
        negbig = T("negbig", [128, 1], F32)
        mskb = T("mskb", [128, 512], BF16)
        neghalf = T("neghalf", [128, 1], F32)
        Rt = T("Rt", [128, 512], F32)
        Rbc = T("Rbc", [128, 512], F32)
        junkA = T("junkA", [128, 1024], BF16)
        zt = T("zt", [128, 1024], F32)
        mixB = T("mixB", [128, 4, 128], BF16)
        Dg = [T(f"Dg{i}", [128, 4, 128], BF16) for i in range(2)]
        smA = [T(f"smA{i}", [128, 160], F32) for i in range(2)]
        sC = T("sC", [128, 2], F32)
        Irep = T("Irep", [128, 512], BF16)
        smB = T("smB", [128, 8], F32)
        smu = T("smu", [128, 4], U32)
        AR = st.enter_context(nc.sbuf_tensor("AR", [128, 12288], F32))

        def AV(off_kb, kb, dt, pattern=None, **kw):
            a = AR[:, off_kb * 256:(off_kb + kb) * 256]
            if dt != F32:
                a = a.bitcast(dt)
            if pattern:
                a = a.rearrange(pattern, **kw)
            return Tl(a, [("AR", g) for g in range(off_kb, off_kb + kb)])

        xt = [AV(0, 8, BF16, "p (k n) -> p k n", k=8), AV(8, 8, BF16, "p (k n) -> p k n", k=8)]
        posi = AV(16, 2, I32)
        ang = AV(18, 2, F32)
        kint = AV(20, 2, I32)
        r1t = AV(22, 2, F32)
        cost = AV(24, 2, F32)
        sint = AV(26, 2, F32)
        t1t = AV(28, 2, F32)
        t2t = AV(30, 2, F32)
        aqT = AV(32, 4, BF16, "p (c n) -> p c n", c=4)
        akO = AV(36, 2, BF16, "p (c n) -> p c n", c=2)
        akP = AV(38, 2, BF16, "p (c n) -> p c n", c=2)
        agT = T("agT", [128, 4, 512], BF16)
        avO = T("avO", [128, 4, 2, 129], BF16)
        avP = T("avP", [128, 4, 2, 129], BF16)
        Esw = [AV(44, 1, BF16), AV(45, 1, BF16)]
        Emsw = [AV(46, 1, BF16), AV(47, 1, BF16)]
        Esw1 = [T(f"Esw1{h}", [128, 512], BF16) for h in range(2)]
        Emsw1 = [T(f"Emsw1{h}", [128, 512], BF16) for h in range(2)]
        IS = [AV(2 * kc, 2, F32) for kc in range(16)]
        cand = T("cand", [128, 64, 16], F32)
        cand.keys = [("cand", c) for c in range(128)]
        rbuf2 = [[T(f"rbuf{p}{h}", [128, 512], BF16) for h in range(4)] for p in range(2)]
        Ee = [AV(40, 1, BF16), AV(41, 1, BF16)]
        Eo = [AV(42, 1, BF16), AV(43, 1, BF16)]
        Eme = [AV(44, 1, BF16), AV(45, 1, BF16)]
        Emo = [AV(46, 1, BF16), AV(47, 1, BF16)]

        ps = []
        for i in range(8):
            p = st.enter_context(nc.psum_tensor(f"ps{i}", [128, 512], F32))
            ps.append(Tl(p[:], [("ps", i)]))

        pg = Prog(nc)
        cs = cst_t.ap

        def dma(q, out_ap, in_ap, reads, writes):
            pg.op(q, lambda e: e.dma_start(out=out_ap, in_=in_ap), reads, writes, dma=True)

        def mmul(out_ap, lhsT, rhs, start, stop, reads, writes):
            pg.op("pe", lambda e: e.matmul(out_ap, lhsT, rhs, start=start, stop=stop), reads, writes)

        def act(out_ap, in_ap, func, reads, writes, bias=0.0, scale=1.0, accum=None):
            pg.op("act", lambda e: e.activation(out_ap, in_ap, func, bias=bias, scale=scale,
                                                accum_out=accum), reads, writes)

        def ts(eng, out_ap, in_ap, s1, s2, op0, op1, reads, writes, accum=None):
            if accum is None:
                pg.op(eng, lambda e: e.tensor_scalar(out_ap, in_ap, s1, s2, op0=op0, op1=op1), reads, writes)
            else:
                pg.op(eng, lambda e: e.tensor_scalar(out_ap, in_ap, s1, s2, op0=op0, op1=op1,
                                                     accum_out=accum), reads, writes)

        def tt(eng, out_ap, a, b, op, reads, writes):
            pg.op(eng, lambda e: e.tensor_tensor(out=out_ap, in0=a, in1=b, op=op), reads, writes)

        def stt(out_ap, in0, scalar, in1, op0, op1, reads, writes):
            pg.op("dve", lambda e: e.scalar_tensor_tensor(out=out_ap, in0=in0, scalar=scalar, in1=in1,
                                                          op0=op0, op1=op1), reads, writes)

        dma("sp", cst_t.ap, cst, [], [cst_t])
        dma("sp", bias_t.ap, biasT, [], [bias_t])
        dma("pool", wo_t.ap.rearrange("p k n -> p (k n)"), wo, [], [wo_t])
        dma("pool", bo_t.ap[0:1, :], bo, [], [bo_t])
        dma("sp", lng_t.ap, lng.to_broadcast([128, 1024]), [], [lng_t])
        dma("sp", lnb_t.ap, lnb.to_broadcast([128, 1024]), [], [lnb_t])
        dma("pool", wvg_t.ap.rearrange("p k n -> p (k n)"), wvg, [], [wvg_t])
        dma("pool", wvo_t.ap.rearrange("p k n -> p (k n)"), wvo, [], [wvo_t])
        dma("sp", bvg_t.ap, bvg.to_broadcast([128, 64]), [], [bvg_t])
        dma("sp", bvo_t.ap, bvo.to_broadcast([128, 132]), [], [bvo_t])
        dma("sp", snk.ap[64:65, :], sinks[0:1, 0:4], [], [snk])
        dma("sp", snk.ap[32:33, :], sinks[0:1, 4:8], [], [snk])
        pg.op("pool", lambda e: e.memset(ident.ap, 0.0), [], [ident])
        pg.op("pool", lambda e: e.affine_select(out=ident.ap, in_=ident.ap, pattern=[[-1, 128]],
                                                compare_op=ALU.not_equal, fill=1.0, base=0,
                                                channel_multiplier=1), [ident], [ident])
        pg.op("pool", lambda e: e.memset(ones_t.ap, 1.0), [], [ones_t])
        for q in range(4):
            pg.op("pool", (lambda o: lambda e: e.tensor_copy(o, ident.ap))(Irep.ap[:, q * 128:(q + 1) * 128]), [ident], [Irep])
        pg.op("pool", lambda e: e.memset(negbig.ap, -1.0e29), [], [negbig])
        pg.op("dve", lambda e: e.tensor_copy(mskb.ap, cs[:, C_SWM:C_SWM + 512]), [cst_t], [mskb])
        pg.op("pool", lambda e: e.memset(neghalf.ap, -0.5), [], [neghalf])
        for _t in bqTs:
            pg.op("pool", (lambda f: lambda e: e.memset(f, 0.0))(_t.ap.rearrange("p h n -> p (h n)")), [], [_t])
        pg.op("pool", lambda e: e.memset(iqT.ap.rearrange("p h n -> p (h n)"), 0.0), [], [iqT])
        pg.op("pool", lambda e: e.memset(Sel.ap, 0.0), [], [Sel])
        pg.op("pool", lambda e: e.memset(Sel.ap[64:65, 0:64], 1.0), [Sel], [Sel])
        pg.op("pool", lambda e: e.memset(Sel.ap[32:33, 64:128], 1.0), [Sel], [Sel])
        pg.op("pool", lambda e: e.memset(Rt.ap, 0.0), [], [Rt])
        pg.op("pool", lambda e: e.memset(onesf.ap, 1.0), [], [onesf])
        for vt in (Vb, avO, avP):
            flat = vt.ap.rearrange("p a b -> p (a b)") if len(vt.ap.shape) == 3 else vt.ap.rearrange("p a g b -> p (a g b)")
            pg.op("pool", (lambda f: lambda e: e.memset(f, 0.0))(flat), [], [vt])
            v3 = vt.ap if len(vt.ap.shape) == 3 else vt.ap.rearrange("p a g b -> p (a g) b")
            pg.op("pool", (lambda f: lambda e: e.memset(f, 1.0))(v3[:, :, 32:33]), [], [vt])
            pg.op("pool", (lambda f: lambda e: e.memset(f, 1.0))(v3[:, :, 128:129]), [], [vt])
        act(snk.ap[64:65, :], snk.ap[64:65, :], AF.Exp, [snk], [snk])
        act(snk.ap[32:33, :], snk.ap[32:33, :], AF.Exp, [snk], [snk])

        wcount = [0]

        wbf_ready = [False]

        def loadW(cid):
            w = Wb[wcount[0] % 4]
            wcount[0] += 1
            if wbf_ready[0]:
                dma("sp", w.ap.rearrange("p k n -> p (k n)"), wbf[cid], [("wbf", cid)], [w])
            else:
                dma("pool", w.ap.rearrange("p k n -> p (k n)"), wT[cid], [], [w])
            return w

        def cast_weights():
            for cid in range(2, 34):
                dma("pool", wbf[cid], wT[cid], [], [("wbf", cid)])
            wbf_ready[0] = True

        def load_x(buf, src, g):
            dma("pool", buf.ap, src[:, :, g * 512:(g + 1) * 512].rearrange("k p n -> p k n"), [], [buf])

        TSETS = [dict(posi=posi, ang=ang, kint=kint, r1t=r1t, cost=cost, sint=sint),
                 dict(posi=AV(32, 2, I32), ang=AV(34, 2, F32), kint=AV(36, 2, I32), r1t=AV(38, 2, F32),
                      cost=AV(40, 2, F32), sint=AV(42, 2, F32))]

        def tables(pos_src, g, tset=0):
            T_ = TSETS[tset]
            posi, ang, kint, r1t, cost, sint = (T_["posi"], T_["ang"], T_["kint"], T_["r1t"], T_["cost"], T_["sint"])
            dma("sp", posi.ap, pos_src[0:1, g * 512:(g + 1) * 512].to_broadcast([128, 512]), [], [posi])
            ts("dve", ang.ap, posi.ap, cs[:, C_INV:C_INV + 1], None, ALU.mult, ALU.bypass, [posi, cst_t], [ang])
            ts("dve", kint.ap, ang.ap, 1.0 / TWO_PI, None, ALU.mult, ALU.bypass, [ang], [kint])
            stt(r1t.ap, kint.ap, -C1, ang.ap, ALU.mult, ALU.add, [kint, ang], [r1t])
            stt(r1t.ap, kint.ap, -C2, r1t.ap, ALU.mult, ALU.add, [kint, r1t], [r1t])
            ts("dve", r1t.ap, r1t.ap, -PI32, PI32, ALU.max, ALU.min, [r1t], [r1t])
            act(sint.ap, r1t.ap, AF.Sin, [r1t, cst_t], [sint], scale=cs[:, C_SGN:C_SGN + 1])
            ts("dve", kint.ap, ang.ap, 1.0 / TWO_PI, 0.25, ALU.mult, ALU.add, [ang], [kint])
            stt(r1t.ap, kint.ap, -C1, ang.ap, ALU.mult, ALU.add, [kint, ang], [r1t])
            stt(r1t.ap, kint.ap, -C2, r1t.ap, ALU.mult, ALU.add, [kint, r1t], [r1t])
            ts("dve", r1t.ap, r1t.ap, -1.5 * PI32, 0.5 * PI32, ALU.max, ALU.min, [r1t], [r1t])
            act(cost.ap, r1t.ap, AF.Sin, [r1t, cst_t], [cost], bias=cs[:, C_HPI:C_HPI + 1], scale=SHR)

        pbank = [0]

        def projT(xtile, w, bank):
            for k in range(8):
                mmul(ps[bank].ap, w.ap[:, k, :], xtile.ap[:, k, :], k == 0, k == 7, [w, xtile], [ps[bank]])

        t1alt = [t1t, AV(44, 2, F32)]
        t2alt = [t2t, AV(46, 2, F32)]

        PAIRS = [(0, 1), (2, 7)]

        def run_jobs(xtile, jobs):
            for _ in run_jobs_gen(xtile, jobs):
                pass

        def run_jobs_gen(xtile, jobs):
            def preload(job):
                return [loadW(job[1]), loadW(job[2])] if job[0] == "rope" else [loadW(job[1])]
            nxt = preload(jobs[0])
            for n, job in enumerate(jobs):
                cur = nxt
                if n + 1 < len(jobs):
                    nxt = preload(jobs[n + 1])
                b0, b1 = PAIRS[pbank[0] % 2]
                pbank[0] += 1
                if job[0] == "rope":
                    _, cid, cidp, out_tl, out_ap = job
                    ta, tb = t1alt[n % 2], t2alt[n % 2]
                    projT(xtile, cur[0], b0)
                    projT(xtile, cur[1], b1)
                    stt(ta.ap, ps[b0].ap, bias_t.ap[:, cid:cid + 1], cost.ap, ALU.add, ALU.mult,
                        [ps[b0], bias_t, cost], [ta])
                    stt(tb.ap, ps[b1].ap, bias_t.ap[:, cidp:cidp + 1], sint.ap, ALU.add, ALU.mult,
                        [ps[b1], bias_t, sint], [tb])
                    if isinstance(out_ap, tuple):
                        tt("pool", out_ap[0], ta.ap[0:64, :], tb.ap[0:64, :], ALU.add, [ta, tb], [out_tl])
                        tt("pool", out_ap[1], ta.ap[64:128, :], tb.ap[64:128, :], ALU.add, [ta, tb], [out_tl])
                    else:
                        tt("pool", out_ap, ta.ap, tb.ap, ALU.add, [ta, tb], [out_tl])
                else:
                    _, cid, out_tl, out_ap = job
                    projT(xtile, cur[0], b0)
                    act(out_ap, ps[b0].ap, AF.Silu, [ps[b0], bias_t], [out_tl], bias=bias_t.ap[:, cid:cid + 1])
                yield

        vbank = [0]

        def vproj(xtile, tt_, wtile, ncol):
            bank = (1, 7)[vbank[0] % 2]
            vbank[0] += 1
            for k in range(8):
                mmul(ps[bank].ap[:, 0:ncol], xtile.ap[:, k, tt_ * 128:(tt_ + 1) * 128], wtile.ap[:, k, :],
                     k == 0, k == 7, [xtile, wtile], [ps[bank]])
            return bank

        WG = [None] * 2
        load_x(xt[0], xg, 0)
        for c in range(2):
            WG[c] = loadW(c)
        for g in range(16):
            xb = xt[g % 2]
            if g + 1 < 16:
                load_x(xt[(g + 1) % 2], xg, g + 1)
            for cid in (2 + 2 * g, 3 + 2 * g):
                dma("pool", wbf[cid], wT[cid], [], [("wbf", cid)])
            tables(posg, g, g % 2)
            gcost, gsint = TSETS[g % 2]["cost"], TSETS[g % 2]["sint"]
            b0 = pbank[0] % 2 * 2
            pbank[0] += 1
            projT(xb, WG[0], b0)
            projT(xb, WG[1], b0 + 1)
            ta, tb = t1alt[g % 2], t2alt[g % 2]
            stt(ta.ap, ps[b0].ap, bias_t.ap[:, 0:1], gcost.ap, ALU.add, ALU.mult, [ps[b0], bias_t, gcost], [ta])
            stt(tb.ap, ps[b0 + 1].ap, bias_t.ap[:, 1:2], gsint.ap, ALU.add, ALU.mult, [ps[b0 + 1], bias_t, gsint], [tb])
            tt("pool", KT.ap[:, g * 512:(g + 1) * 512], ta.ap, tb.ap, ALU.add, [ta, tb], [KT])
            for t4 in range(4):
                bank = vproj(xb, t4, wvg_t, 64)
                tt("dve", Vb.ap[:, g * 4 + t4, 64:128], ps[bank].ap[:, 0:64], bvg_t.ap, ALU.add,
                   [ps[bank], bvg_t], [Vb])

        def normalize(bank_e, bank_o, gate_tl, gate_ap, out_tl, out_ap, with_sink, bb0):
            pe_, po_ = ps[bank_e], ps[bank_o]
            if with_sink:
                tt("dve", Rt.ap[64:65, :].rearrange("p (h t) -> p h t", h=4),
                   pe_.ap[64:65, :].rearrange("p (h t) -> p h t", h=4),
                   snk.ap[64:65, :].unsqueeze(2).to_broadcast([1, 4, 128]), ALU.add, [pe_, snk], [Rt])
                tt("dve", Rt.ap[32:33, :].rearrange("p (h t) -> p h t", h=4),
                   po_.ap[32:33, :].rearrange("p (h t) -> p h t", h=4),
                   snk.ap[32:33, :].unsqueeze(2).to_broadcast([1, 4, 128]), ALU.add, [po_, snk], [Rt])
            else:
                act(Rt.ap[64:65, :], pe_.ap[64:65, :], AF.Identity, [pe_], [Rt])
                act(Rt.ap[32:33, :], po_.ap[32:33, :], AF.Identity, [po_], [Rt])
            mmul(ps[bb0].ap, Sel.ap, Rt.ap, True, True, [Sel, Rt], [ps[bb0]])
            pg.op("dve", lambda e: e.reciprocal(Rbc.ap, ps[bb0].ap), [ps[bb0]], [Rbc])
            tt("dve", Rbc.ap[0:64, :], pe_.ap[0:64, :], Rbc.ap[0:64, :], ALU.mult, [pe_, Rbc], [Rbc])
            tt("dve", Rbc.ap[64:128, :], po_.ap[64:128, :], Rbc.ap[64:128, :], ALU.mult, [po_, Rbc], [Rbc])
            tt("pool", out_ap, Rbc.ap.rearrange("p (h t) -> p h t", h=4), gate_ap, ALU.mult,
               [Rbc, gate_tl], [out_tl])

        def stageA(i, bl):
            NC_ = i + 1
            CS = 8 * NC_
            par = i % 2
            sA, dg = smA[par], Dg[par]
            tcols = slice(bl * 128, (bl + 1) * 128)
            wv = wtok.ap[:, bl, :]
            absw, sgn = sA.ap[:, 0:4], sA.ap[:, 4:8]
            ts("dve", absw, wv, -1.0, None, ALU.mult, ALU.bypass, [wtok], [sA])
            tt("dve", absw, absw, wv, ALU.max, [wtok, sA], [sA])
            ts("dve", sgn, wv, 0.0, 2.0, ALU.is_ge, ALU.mult, [wtok], [sA])
            ts("dve", sgn, sgn, -1.0, None, ALU.add, ALU.bypass, [sA], [sA])
            for h in range(4):
                ts("dve", dg.ap[:, h, :], ident.ap, sA.ap[:, 4 + h:5 + h], None, ALU.mult, ALU.bypass,
                   [ident, sA], [dg])
            yield
            for kc in range(NC_):
                pb = [(0, 1, 2, 7)[(h + kc) % 4] for h in range(4)]
                rbuf = rbuf2[kc % 2]
                for h in range(4):
                    mmul(ps[pb[h]].ap, iqT.ap[:, h, tcols], KT.ap[:, kc * 512:(kc + 1) * 512],
                         True, True, [iqT, KT], [ps[pb[h]]])
                yield
                for h in range(4):
                    if h % 2 == 0:
                        act(rbuf[h].ap, ps[pb[h]].ap, AF.Relu, [ps[pb[h]], sA], [rbuf[h]], scale=sA.ap[:, h:h + 1])
                    else:
                        ts("dve", rbuf[h].ap, ps[pb[h]].ap, sA.ap[:, h:h + 1], 0.0, ALU.mult, ALU.max,
                           [ps[pb[h]], sA], [rbuf[h]])
                yield
                ab = pb[0]
                for h in range(4):
                    mmul(ps[ab].ap, dg.ap[:, h, :], rbuf[h].ap, h == 0, h == 3, [dg, rbuf[h]], [ps[ab]])
                stt(IS[kc].ap, ps[ab].ap, float(-1e-30 * 512 * kc), cs[:, C_EPS:C_EPS + 512], ALU.add, ALU.add,
                    [ps[ab], cst_t], [IS[kc]])
                if kc == NC_ - 1:
                    tt("dve", IS[kc].ap, IS[kc].ap, cs[:, C_CNEG:C_CNEG + 512], ALU.min, [IS[kc], cst_t], [IS[kc]])
                yield
            isflat = AR[:, 0:512 * NC_]
            iskeys = [IS[k] for k in range(NC_)]
            cflat = cand.ap.rearrange("p c e -> p (c e)")
            nlo, hi, wdt, nmid, cnt, lo = (sA.ap[:, 8:9], sA.ap[:, 9:10], sA.ap[:, 10:11], sA.ap[:, 11:12],
                                            sA.ap[:, 12:13], sA.ap[:, 13:14])
            if NC_ <= 2:
                nbis = NBIS + 4
                cflat = isflat
                ckeys = iskeys
                jk = junkA.ap[:, 0:512 * NC_]
                pg.op("dve", lambda e: e.tensor_reduce(out=hi, in_=isflat, op=ALU.max, axis=mybir.AxisListType.X), iskeys, [sA])
                ts("dve", cand.ap.rearrange("p c e -> p (c e)")[:, 0:512 * NC_], isflat, -1.0e29, 3.0e30, ALU.is_lt, ALU.mult,
                   iskeys, [cand])
                tt("dve", cand.ap.rearrange("p c e -> p (c e)")[:, 0:512 * NC_], cand.ap.rearrange("p c e -> p (c e)")[:, 0:512 * NC_],
                   isflat, ALU.add, iskeys + [cand], [cand])
                pg.op("dve", lambda e: e.tensor_reduce(out=lo, in_=cand.ap.rearrange("p c e -> p (c e)")[:, 0:512 * NC_],
                                                       op=ALU.min, axis=mybir.AxisListType.X), [cand], [sA])
            else:
                nbis = NBIS
                ckeys = [cand]
                jk = junkA.ap
                cand8 = cflat.rearrange("p (c e) -> p c e", e=8)
                CS2 = CS // 2
                for c in range(128):
                    src = isflat[:, c * CS2:(c + 1) * CS2]
                    pg.op("dve", (lambda o, s_: lambda e: e.max(out=o, in_=s_))(cand8[:, c, :], src), iskeys, [("cand", c)])
                    if c % 4 == 3:
                        yield
                chi = cand8[:, :, 1:2].rearrange("p c e -> p (c e)")
                clo = cand8[:, :, 3:4].rearrange("p c e -> p (c e)")
                tv = sA.ap[:, 32:160]
                pg.op("dve", lambda e: e.tensor_reduce(out=hi, in_=chi, op=ALU.max, axis=mybir.AxisListType.X), [cand], [sA])
                ts("dve", tv, clo, -1.0e29, 3.0e30, ALU.is_lt, ALU.mult, [cand], [sA])
                tt("dve", tv, tv, clo, ALU.add, [cand, sA], [sA])
                pg.op("dve", lambda e: e.tensor_reduce(out=lo, in_=tv, op=ALU.min, axis=mybir.AxisListType.X), [sA], [sA])
            tt("dve", wdt, hi, lo, ALU.subtract, [sA], [sA])
            yield
            mid = nmid
            for it in range(nbis):
                stt(mid, wdt, float(0.5 ** (it + 1)), lo, ALU.mult, ALU.add, [sA], [sA])
                ts("dve", jk, cflat, mid, None, ALU.is_ge, ALU.add, ckeys + [sA, junkA], [junkA, sA], accum=cnt)
                ts("dve", smu.ap[:, 0:1], cnt, 255.5, None, ALU.is_gt, ALU.bypass, [sA], [smu])
                pg.op("dve", lambda e: e.copy_predicated(lo, smu.ap[:, 0:1], mid), [sA, smu], [sA])
                yield
            for kc in range(NC_):
                ts("dve", MA[kc].ap, IS[kc].ap, lo, -32768.0, ALU.is_lt, ALU.mult, [IS[kc], sA], [MA[kc]])
            yield

        def unitsA(i):
            return (1 + 3 * (i + 1) + 1 + NBIS + 4 + 1) if i <= 1 else (1 + 3 * (i + 1) + 32 + 1 + NBIS + 1)

        def stageB(i, bl):
            NC_ = i + 1
            NK = 4 * NC_
            tcols = slice(bl * 128, (bl + 1) * 128)
            dma("sp", zt.ap, xtok[i * 128:(i + 1) * 128, :], [], [zt])
            bqT = bqTs[(i // 4) % 2]
            bgT = bgTs[(i // 4) % 2]
            bq2 = bqT.ap.rearrange("p (m two) n -> p two m n", two=2)
            Ebuf = [Ee[0], Ee[1], Eo[0], Eo[1]]

            for half in range(2):
                def qk(kt):
                    ks = slice(kt * 128, (kt + 1) * 128)
                    bank = 3 + kt % 2
                    mmul(ps[bank].ap, KT.ap[:, ks], bq2[:, half][:, :, tcols], True, False, [KT, bqT], [ps[bank]])
                    mmul(ps[bank].ap, MA[kt // 4].ap[:, (kt % 4) * 128:(kt % 4 + 1) * 128], Irep.ap, False, True,
                         [MA[kt // 4], Irep], [ps[bank]])

                def pv(kt):
                    em = Ebuf[kt % 4]
                    if half == 0:
                        mmul(ps[5].ap[0:65, :], Vb.ap[:, kt, 64:129], em.ap, kt == 0, kt == NK - 1, [Vb, em], [ps[5]])
                    else:
                        mmul(ps[6].ap, Vb.ap[:, kt, 0:128], em.ap, kt == 0, kt == NK - 1, [Vb, em], [ps[6]])

                qk(0)
                for kt in range(NK):
                    if kt + 1 < NK:
                        qk(kt + 1)
                    if kt >= 1:
                        pv(kt - 1)
                    eb = Ebuf[kt % 4]
                    act(eb.ap, ps[3 + kt % 2].ap, AF.Exp, [ps[3 + kt % 2]], [eb], scale=0.125)
                    yield
                pv(NK - 1)
                yield
            normalize(5, 6, bgT, bgT.ap[:, :, tcols], mixB, mixB.ap, False, 3)
            yield
            for hf in range(2):
                bank = 3 + hf
                cols = slice(hf * 512, (hf + 1) * 512)
                for c in range(8):
                    lhsT = mixA.ap[:, c, tcols] if c < 4 else mixB.ap[:, c - 4, :]
                    mmul(ps[bank].ap, lhsT, wo_t.ap[:, c, cols], c == 0, False, [mixA, mixB, wo_t], [ps[bank]])
                mmul(ps[bank].ap, ones_t.ap[0:1, :], bo_t.ap[0:1, cols], False, True, [ones_t, bo_t], [ps[bank]])
                stt(zt.ap[:, cols], zt.ap[:, cols], ALPHA, ps[bank].ap, ALU.mult, ALU.add, [zt, ps[bank]], [zt])
            yield
            ssum, ssq, mean, var = smB.ap[:, 0:1], smB.ap[:, 1:2], smB.ap[:, 2:3], smB.ap[:, 3:4]
            pg.op("pool", lambda e: e.memset(smB.ap[:, 0:2], 0.0), [], [smB])
            act(junkA.ap, zt.ap, AF.Identity, [zt, junkA], [junkA, smB], accum=ssum)
            act(junkA.ap, zt.ap, AF.Square, [zt, junkA], [junkA, smB], accum=ssq)
            ts("dve", mean, ssum, 1.0 / 1024, None, ALU.mult, ALU.bypass, [smB], [smB])
            stt(var, mean, -1.0, mean, ALU.mult, ALU.mult, [smB], [smB])
            stt(var, ssq, 1.0 / 1024, var, ALU.mult, ALU.add, [smB], [smB])
            ts("dve", var, var, LN_EPS, None, ALU.add, ALU.bypass, [smB], [smB])
            pg.op("pool", lambda e: e.tensor_tensor(out=var, in0=var, in1=neghalf.ap, op=ALU.pow), [smB, neghalf], [smB])
            ts("dve", zt.ap, zt.ap, mean, var, ALU.subtract, ALU.mult, [zt, smB], [zt])
            tt("pool", zt.ap, zt.ap, lng_t.ap, ALU.mult, [zt, lng_t], [zt])
            tt("pool", zt.ap, zt.ap, lnb_t.ap, ALU.add, [zt, lnb_t], [zt])
            dma("sp", out[i * 128:(i + 1) * 128, :], zt.ap, [zt], [])
            yield

        def unitsB(i):
            return 8 * (i + 1) + 6

        def drain(g):
            for _ in g:
                pass

        def merge(ga, na, gb, nb):
            da = db = 0
            alive_a, alive_b = ga is not None, True
            while alive_a or alive_b:
                take_a = alive_a and (not alive_b or da * nb <= db * na)
                if take_a:
                    try:
                        next(ga)
                        da += 1
                    except StopIteration:
                        alive_a = False
                else:
                    try:
                        next(gb)
                        db += 1
                    except StopIteration:
                        alive_b = False

        wbf_ready[0] = True

        def opass_gen(og):
            bqT, bgT = bqTs[og % 2], bgTs[og % 2]
            xb = xt[0]
            load_x(xb, xo, og)
            load_x(xt[1], xp, og)
            tables(poso, og)
            yield
            jobs = []
            jobs += [("rope", 10 + m, 12 + m, iqT, (iqT.ap[64:128, 2 * m, :], iqT.ap[64:128, 2 * m + 1, :])) for m in range(2)]
            jobs += [("rope", 14 + m, 18 + m, aqT, aqT.ap[:, m, :]) for m in range(4)]
            jobs += [("rope", 22 + m, 24 + m, akO, akO.ap[:, m, :]) for m in range(2)]
            jobs += [("rope", 2 + m, 6 + m, bqT, (bqT.ap[0:64, 2 * m, :], bqT.ap[0:64, 2 * m + 1, :])) for m in range(4)]
            jobs += [("gate", 26 + m, agT, agT.ap[:, m, :]) for m in range(4)]
            jobs += [("gate", 30 + m, bgT, bgT.ap[:, m, :]) for m in range(4)]
            for _ in run_jobs_gen(xb, jobs):
                yield
            for t4 in range(4):
                bank = vproj(xb, t4, wvo_t, 132)
                tt("dve", avO.ap[:, t4, :, 64:128], ps[bank].ap[:, 0:128].rearrange("p (g d) -> p g d", g=2),
                   bvo_t.ap[:, 0:128].rearrange("p (g d) -> p g d", g=2), ALU.add, [ps[bank], bvo_t], [avO])
                tt("dve", wtok.ap[:, t4, :], ps[bank].ap[:, 128:132], bvo_t.ap[:, 128:132], ALU.add,
                   [ps[bank], bvo_t], [wtok])
                ts("dve", wtok.ap[:, t4, :], wtok.ap[:, t4, :], 1.0 / 16.0, None, ALU.mult, ALU.bypass, [wtok], [wtok])
                yield
            xb = xt[1]
            tables(posp, og)
            yield
            for _ in run_jobs_gen(xb, [("rope", 22 + m, 24 + m, akP, akP.ap[:, m, :]) for m in range(2)]):
                yield
            for t4 in range(4):
                bank = vproj(xb, t4, Tl(wvo_t.ap[:, :, 0:128], wvo_t.keys), 128)
                tt("dve", avP.ap[:, t4, :, 64:128], ps[bank].ap[:, 0:128].rearrange("p (g d) -> p g d", g=2),
                   bvo_t.ap[:, 0:128].rearrange("p (g d) -> p g d", g=2), ALU.add, [ps[bank], bvo_t], [avP])
                yield

        NOPASS = 1 + 20 + 4 + 1 + 2 + 4

        for og in range(4):
            def swa_gen(og=og, piped=(og != 0)):
                def banks(bl, g):
                    if not piped:
                        return (3, 4), (5, 6), 3, (Esw, Emsw)
                    sb = (3, 4) if g == 0 else (2, 7)
                    ob = (5, 6) if bl % 2 == 0 else (0, 1)
                    return sb, ob, (7 if bl % 2 == 0 else 2), ((Esw, Emsw) if g == 0 else (Esw1, Emsw1))

                def s1(bl, g):
                    sb, _, _, _ = banks(bl, g)
                    tcols = slice(bl * 128, (bl + 1) * 128)
                    for half in range(2):
                        rows = slice(64 * half, 64 * half + 64)
                        for kt2, aksrc in enumerate((akP, akO)):
                            mmul(ps[sb[half]].ap[:, kt2 * 256:(kt2 + 1) * 256],
                                 aksrc.ap[rows, g, tcols], aqT.ap[rows, 2 * g:2 * g + 2, tcols], True, True,
                                 [aksrc, aqT], [ps[sb[half]]])

                def s2(bl, g):
                    sb, _, _, (E_, Em_) = banks(bl, g)
                    mcol = 256 if (og == 0 and bl == 0) else 0
                    for half in range(2):
                        act(E_[half].ap, ps[sb[half]].ap, AF.Exp, [ps[sb[half]]], [E_[half]], scale=0.125)
                        e4 = E_[half].ap.rearrange("p (k h t) -> p k h t", k=2, h=2)
                        o4 = Em_[half].ap.rearrange("p (k h t) -> p k h t", k=2, h=2)
                        for kt2 in range(2):
                            mk = bc_mid(mskb.ap[:, mcol + kt2 * 128:mcol + kt2 * 128 + 128], 2)
                            tt("dve", o4[:, kt2], e4[:, kt2], mk, ALU.mult, [E_[half], mskb], [Em_[half]])

                def s3(bl, g):
                    _, ob, _, (E_, Em_) = banks(bl, g)
                    for kt2, avsrc in enumerate((avP, avO)):
                        mmul(ps[ob[0]].ap[0:65, g * 256:(g + 1) * 256], avsrc.ap[:, bl, g, 64:129],
                             Em_[0].ap[:, kt2 * 256:(kt2 + 1) * 256],
                             kt2 == 0, kt2 == 1, [avsrc, Em_[0]], [ps[ob[0]]])
                        mmul(ps[ob[1]].ap[:, g * 256:(g + 1) * 256], avsrc.ap[:, bl, g, 0:128],
                             Em_[1].ap[:, kt2 * 256:(kt2 + 1) * 256],
                             kt2 == 0, kt2 == 1, [avsrc, Em_[1]], [ps[ob[1]]])

                def s4(bl):
                    _, ob, bb0, _ = banks(bl, 0)
                    tcols = slice(bl * 128, (bl + 1) * 128)
                    normalize(ob[0], ob[1], agT, agT.ap[:, :, tcols], mixA, mixA.ap[:, :, tcols], True, bb0)

                if not piped:
                    for bl in range(4):
                        for g in range(2):
                            s1(bl, g)
                            s2(bl, g)
                            s3(bl, g)
                            yield
                        s4(bl)
                        yield
                else:
                    s1(0, 0)
                    s1(0, 1)
                    s2(0, 0)
                    s2(0, 1)
                    for bl in range(4):
                        if bl + 1 < 4:
                            s1(bl + 1, 0)
                            s1(bl + 1, 1)
                        s3(bl, 0)
                        s3(bl, 1)
                        if bl + 1 < 4:
                            s2(bl + 1, 0)
                            s2(bl + 1, 1)
                        s4(bl)
                        yield

            if og == 0:
                drain(opass_gen(0))
                drain(stageA(0, 0))
                drain(swa_gen(piped=True))
            else:
                drain(swa_gen())
            for bl in range(4):
                i = og * 4 + bl
                if bl < 3:
                    merge(stageA(i + 1, bl + 1), unitsA(i + 1), stageB(i, bl), unitsB(i))
                elif og < 3:
                    drain(opass_gen(og + 1))
                    merge(stageA(i + 1, 0), unitsA(i + 1), stageB(i, bl), unitsB(i))
                else:
                    drain(stageB(i, bl))

        counts = pg.emit(st)
    return nc, counts


_CACHE = {}


def _chunks():
    AQ, AK, AV_, AG, BQ, BK, BV, BG, IQ, IK, IW = 0, 512, 640, 768, 1280, 1792, 1856, 1920, 2432, 2688, 2752
    perm = (np.arange(64) + 32) % 64
    r64 = np.arange(64)

    def heads(base, hl, p):
        return np.concatenate([base + 64 * h + (perm if p else r64) for h in hl])

    ch = []
    ch += [np.concatenate([BK + r64, IK + r64]), np.concatenate([BK + perm, IK + perm])]
    ch += [heads(BQ, [2 * m, 2 * m + 1], False) for m in range(4)]
    ch += [heads(BQ, [2 * m, 2 * m + 1], True) for m in range(4)]
    ch += [heads(IQ, [2 * m, 2 * m + 1], False) for m in range(2)]
    ch += [heads(IQ, [2 * m, 2 * m + 1], True) for m in range(2)]
    ch += [heads(AQ, [2 * m, 2 * m + 1], False) for m in range(4)]
    ch += [heads(AQ, [2 * m, 2 * m + 1], True) for m in range(4)]
    ch += [heads(AK, [g, g], False) for g in range(2)]
    ch += [heads(AK, [g, g], True) for g in range(2)]
    ch += [AG + 128 * m + np.arange(128) for m in range(4)]
    ch += [BG + 128 * m + np.arange(128) for m in range(4)]
    return ch, (AV_, BV, IW)


def kernel(x, positions, w_in, b_in, swa_sinks, w_out, b_out, ln_gain, ln_bias):
    x = np.asarray(x, dtype=np.float32)
    positions = np.asarray(positions, dtype=np.int32)
    w = np.asarray(w_in, dtype=np.float32)[0]
    b = np.asarray(b_in, dtype=np.float32)[0]
    sk = np.asarray(swa_sinks, dtype=np.float32)[0]
    wo_ = np.asarray(w_out, dtype=np.float32)[0]
    bo_ = np.asarray(b_out, dtype=np.float32)
    lg = np.asarray(ln_gain, dtype=np.float32)
    lb = np.asarray(ln_bias, dtype=np.float32)

    if "nc" not in _CACHE:
        _CACHE["nc"] = build()[0]
    nc = _CACHE["nc"]

    ch, (AV_, BV, IW) = _chunks()
    wT = np.stack([np.ascontiguousarray(w[:, c].reshape(8, 128, 128).transpose(1, 0, 2)).reshape(128, 1024) for c in ch])
    biasT = np.ascontiguousarray(np.stack([b[c] for c in ch], axis=1))
    cg = BV + np.arange(64)
    co = np.concatenate([AV_ + np.arange(128), IW + np.arange(4)])
    wvg = np.ascontiguousarray(w[:, cg].reshape(8, 128, 64).transpose(1, 0, 2)).reshape(128, 512)
    wvo = np.ascontiguousarray(w[:, co].reshape(8, 128, 132).transpose(1, 0, 2)).reshape(128, 8 * 132)
    bvg = np.ascontiguousarray(b[cg][None, :])
    bvo = np.ascontiguousarray(b[co][None, :])
    wo_h = np.ascontiguousarray(wo_.reshape(8, 128, 1024).transpose(1, 0, 2)).reshape(128, 8192)
    sinks = np.ascontiguousarray(np.concatenate([sk[0::2], sk[1::2]])[None, :])

    inv = (np.float32(10000.0) ** (-np.arange(0, 64, 2, dtype=np.float32) / np.float32(64))).astype(np.float32)
    pidx = np.arange(128) % 64
    cst_base = np.zeros((128, C_END), np.float32)
    cst_base[:, C_INV] = inv[pidx % 32]
    cst_base[:, C_SGN] = np.where(pidx < 32, -1.0, 1.0) * SHR
    cst_base[:, C_HPI] = (math.pi / 2) * SHR
    cst_base[:, C_EPS:C_EPS + 512] = (-1e-30 * np.arange(512, dtype=np.float64)).astype(np.float32)[None, :]
    s_i = np.arange(128)[:, None]
    t_i = np.arange(128)[None, :]
    swm = np.concatenate([(s_i > t_i).astype(np.float32), (s_i <= t_i).astype(np.float32)], axis=1)

    in_maps = []
    own_idx = []
    for c in range(8):
        bb, j = c // 4, c % 4
        blocks = 4 * np.arange(16) + j
        oi = (blocks[:, None] * 128 + np.arange(128)[None, :]).reshape(-1)
        pb = np.maximum(blocks - 1, 0)
        pi = (pb[:, None] * 128 + np.arange(128)[None, :]).reshape(-1)
        own_idx.append(oi)
        xT = np.ascontiguousarray(x[bb].T).reshape(8, 128, S)
        cst = cst_base.copy()
        cn = np.full((128, 512), -1e30, np.float32)
        for o in range(4):
            if o < j:
                cn[:, o * 128:(o + 1) * 128] = 1e30
            elif o == j:
                cn[:, o * 128:(o + 1) * 128] = np.where(s_i.T <= t_i.T, 1e30, -1e30)
        cst[:, C_CNEG:C_CNEG + 512] = cn
        cst[:, C_SWM:C_SWM + 256] = swm
        swf = swm.copy()
        if j == 0:
            swf[:, 0:128] = 0.0
        cst[:, C_SWMF:C_SWMF + 256] = swf
        in_maps.append({
            "xg": xT,
            "xo": np.ascontiguousarray(xT[:, :, oi]),
            "xp": np.ascontiguousarray(xT[:, :, pi]),
            "xtok": np.ascontiguousarray(x[bb][oi]),
            "posg": np.ascontiguousarray(positions[bb][None, :]),
            "poso": np.ascontiguousarray(positions[bb][oi][None, :]),
            "posp": np.ascontiguousarray(positions[bb][pi][None, :]),
            "wT": wT, "biasT": biasT, "wvg": wvg, "wvo": wvo, "bvg": bvg, "bvo": bvo,
            "wo": wo_h, "bo": bo_, "lng": lg, "lnb": lb, "sinks": sinks, "cst": cst,
        })
    res = run_bass_kernel_spmd(nc, in_maps, core_ids=list(range(8)))
    outp = np.empty((2, S, 1024), np.float32)
    for c in range(8):
        outp[c // 4, own_idx[c]] = res.results[c]["out"]
    return outp
```
